# Optimizing a Trainium2 kernel written in Bass

```python
import math
import jax, jax.numpy as jnp
from jax import lax
import numpy as np

D_MODEL = 1024
BATCH = 8
SEQ = 8192
DEPTH = 4

N_MIXERS = 2
HEAD_DIM = 64
A_Q_HEADS = 12
A_KV_HEADS = 4
A_GROUP = A_Q_HEADS // A_KV_HEADS
WINDOW = 128
BLOCK = 128
SPAN = BLOCK + 2 * WINDOW
B_HEADS = 12
GRID_W = 64
NB_ROWS_MAX = 8
NB_COLS = 16
MEM_HEADS = 4
MEM_LEN = 256
N_BUCKETS = 32
MAX_DISTANCE = 128
D_FF = -(-8 * D_MODEL // (3 * 256)) * 256
RMS_EPS = 1e-6
A_IN = (A_Q_HEADS + 2 * A_KV_HEADS + MEM_HEADS) * HEAD_DIM
B_IN = (3 * B_HEADS + MEM_HEADS) * HEAD_DIM
MIX_WIDTH = (A_Q_HEADS + MEM_HEADS) * HEAD_DIM
NEG = -1e30

kernel_name = 'hybrid_window_gqa_natten_memory_encoder'


def rms_norm(x, g):
    xf = x.astype(jnp.float32)
    y = xf * lax.rsqrt(jnp.mean(xf * xf, axis=-1, keepdims=True) + RMS_EPS)
    return (y * g.astype(jnp.float32)).astype(x.dtype)


def t5_bucket(rel):
    half = N_BUCKETS // 2
    max_exact = half // 2
    n = -rel
    ret = jnp.where(n < 0, half, 0)
    n = jnp.abs(n)
    nf = jnp.maximum(n, 1).astype(jnp.float32)
    large = max_exact + (jnp.log(nf / max_exact) / math.log(MAX_DISTANCE / max_exact)
                         * (half - max_exact)).astype(jnp.int32)
    large = jnp.minimum(large, half - 1)
    return ret + jnp.where(n < max_exact, n, large)


def windowed_gqa(q, k, v, sink, t5_table):
    B, S = q.shape[0], q.shape[1]
    nblk = S // BLOCK
    scale = HEAD_DIM ** -0.5
    kp = jnp.pad(k, ((0, 0), (WINDOW, WINDOW), (0, 0), (0, 0)))
    vp = jnp.pad(v, ((0, 0), (WINDOW, WINDOW), (0, 0), (0, 0)))
    rel = jnp.arange(SPAN)[None, :] - WINDOW - jnp.arange(BLOCK)[:, None]
    in_window = jnp.abs(rel) <= WINDOW
    bias = t5_table[t5_bucket(rel)].astype(jnp.float32)
    bias = bias.transpose(2, 0, 1).reshape(A_KV_HEADS, A_GROUP, BLOCK, SPAN)
    sink_l = jnp.broadcast_to(sink.astype(jnp.float32).reshape(1, A_KV_HEADS, A_GROUP, 1, 1),
                              (B, A_KV_HEADS, A_GROUP, BLOCK, 1))
    qb = q.reshape(B, nblk, BLOCK, A_KV_HEADS, A_GROUP, HEAD_DIM).transpose(1, 0, 2, 3, 4, 5)

    def one_block(args):
        i, q_blk = args
        start = i * BLOCK
        k_blk = lax.dynamic_slice_in_dim(kp, start, SPAN, axis=1)
        v_blk = lax.dynamic_slice_in_dim(vp, start, SPAN, axis=1)
        kpos = start - WINDOW + jnp.arange(SPAN)
        valid = in_window & ((kpos >= 0) & (kpos < S))[None, :]
        s = jnp.einsum('bqkgd,bskd->bkgqs', q_blk, k_blk,
                       preferred_element_type=jnp.float32) * scale + bias
        s = jnp.where(valid, s, NEG)
        p = jax.nn.softmax(jnp.concatenate([s, sink_l], axis=-1), axis=-1)[..., :SPAN]
        return jnp.einsum('bkgqs,bskd->bqkgd', p.astype(v_blk.dtype), v_blk)

    out = lax.map(one_block, (jnp.arange(nblk), qb))
    return out.transpose(1, 0, 2, 3, 4, 5).reshape(B, S, A_Q_HEADS * HEAD_DIM)


def neighbourhood_attn(q, k, v, rpb):
    B, S = q.shape[0], q.shape[1]
    rows = S // GRID_W
    kr = min(NB_ROWS_MAX, rows)
    kc = NB_COLS
    scale = HEAD_DIM ** -0.5
    qg = q.reshape(B, rows, GRID_W, B_HEADS, HEAD_DIM).transpose(1, 0, 2, 3, 4)
    kg = k.reshape(B, rows, GRID_W, B_HEADS, HEAD_DIM)
    vg = v.reshape(B, rows, GRID_W, B_HEADS, HEAD_DIM)
    row_start = jnp.clip(jnp.arange(rows) - kr // 2, 0, rows - kr)
    c_idx = jnp.arange(GRID_W)
    col_start = jnp.clip(c_idx - kc // 2, 0, GRID_W - kc)
    col_nb = col_start[:, None] + jnp.arange(kc)[None, :]
    col_bias_idx = col_nb - c_idx[:, None] + NB_COLS - 1

    def one_row(args):
        r, q_row = args
        rs = row_start[r]
        k_rows = lax.dynamic_slice_in_dim(kg, rs, kr, axis=1)
        v_rows = lax.dynamic_slice_in_dim(vg, rs, kr, axis=1)
        k_nb = k_rows[:, :, col_nb]
        v_nb = v_rows[:, :, col_nb]
        row_bias_idx = rs + jnp.arange(kr) - r + NB_ROWS_MAX - 1
        bias = rpb[:, row_bias_idx[:, None, None], col_bias_idx[None, :, :]]
        bias = bias.astype(jnp.float32).transpose(0, 2, 1, 3)[None]
        s = jnp.einsum('bwhd,brwchd->bhwrc', q_row, k_nb,
                       preferred_element_type=jnp.float32) * scale + bias
        p = jax.nn.softmax(s.reshape(B, B_HEADS, GRID_W, kr * kc), axis=-1)
        p = p.reshape(B, B_HEADS, GRID_W, kr, kc).astype(v_nb.dtype)
        return jnp.einsum('bhwrc,brwchd->bwhd', p, v_nb)

    out = lax.map(one_row, (jnp.arange(rows), qg))
    return out.transpose(1, 0, 2, 3, 4).reshape(B, S, B_HEADS * HEAD_DIM)


def memory_attn(q, mk, mv):
    B, S = q.shape[0], q.shape[1]
    s = jnp.einsum('bshd,bmhd->bhsm', q, mk, preferred_element_type=jnp.float32) * (HEAD_DIM ** -0.5)
    p = jax.nn.softmax(s, axis=-1).astype(mv.dtype)
    return jnp.einsum('bhsm,bmhd->bshd', p, mv).reshape(B, S, MEM_HEADS * HEAD_DIM)


def setup_inputs(seed: int = 0) -> dict:
    key = jax.random.key(seed)
    ks = jax.random.split(key, 16)
    n_a = (DEPTH + N_MIXERS - 1) // N_MIXERS
    n_b = (DEPTH + N_MIXERS - 2) // N_MIXERS

    def dense(k, shape, fan_in):
        return jax.random.normal(k, shape, jnp.float32) * fan_in ** -0.5

    def gain(k, shape):
        return 1.0 + 0.05 * jax.random.normal(k, shape, jnp.float32)

    return {
        'x': jax.random.normal(ks[0], (BATCH, SEQ, D_MODEL), jnp.float32),
        'mem': jax.random.normal(ks[1], (BATCH, MEM_LEN, D_MODEL), jnp.float32),
        'w_in_a': dense(ks[2], (n_a, D_MODEL, A_IN), D_MODEL),
        'sink_a': 0.5 * jax.random.normal(ks[3], (n_a, A_Q_HEADS), jnp.float32),
        'w_in_b': dense(ks[4], (n_b, D_MODEL, B_IN), D_MODEL),
        'rpb_b': 0.5 * jax.random.normal(ks[5], (n_b, B_HEADS, 2 * NB_ROWS_MAX - 1, 2 * NB_COLS - 1), jnp.float32),
        't5_table': 0.5 * jax.random.normal(ks[6], (N_BUCKETS, A_Q_HEADS), jnp.float32),
        'w_mem_kv': dense(ks[7], (DEPTH, D_MODEL, 2 * MEM_HEADS * HEAD_DIM), D_MODEL),
        'w_out': dense(ks[8], (DEPTH, MIX_WIDTH, D_MODEL), MIX_WIDTH),
        'w_gu': dense(ks[9], (DEPTH, D_MODEL, 2 * D_FF), D_MODEL),
        'w_down': dense(ks[10], (DEPTH, D_FF, D_MODEL), D_FF),
        'norm_mix_pre': gain(ks[11], (DEPTH, D_MODEL)),
        'norm_mix_post': gain(ks[12], (DEPTH, D_MODEL)),
        'norm_mem': gain(ks[13], (DEPTH, D_MODEL)),
        'norm_ffn_pre': gain(ks[14], (DEPTH, D_MODEL)),
        'norm_ffn_post': gain(ks[15], (DEPTH, D_MODEL)),
    }


def reference(x, mem, w_in_a, sink_a, w_in_b, rpb_b, t5_table, w_mem_kv, w_out, w_gu, w_down,
              norm_mix_pre, norm_mix_post, norm_mem, norm_ffn_pre, norm_ffn_post):
    B, S = x.shape[0], x.shape[1]
    hd = HEAD_DIM
    for i in range(DEPTH):
        h = rms_norm(x, norm_mix_pre[i])
        m = rms_norm(mem, norm_mem[i])
        mkv = jnp.einsum('bmd,de->bme', m, w_mem_kv[i])
        mk = mkv[..., :MEM_HEADS * hd].reshape(B, MEM_LEN, MEM_HEADS, hd)
        mv = mkv[..., MEM_HEADS * hd:].reshape(B, MEM_LEN, MEM_HEADS, hd)
        if i % N_MIXERS == 0:
            j = i // N_MIXERS
            proj = jnp.einsum('bsd,de->bse', h, w_in_a[j])
            c1 = A_Q_HEADS * hd
            c2 = c1 + A_KV_HEADS * hd
            c3 = c2 + A_KV_HEADS * hd
            q = proj[..., :c1].reshape(B, S, A_Q_HEADS, hd)
            k = proj[..., c1:c2].reshape(B, S, A_KV_HEADS, hd)
            v = proj[..., c2:c3].reshape(B, S, A_KV_HEADS, hd)
            qm = proj[..., c3:].reshape(B, S, MEM_HEADS, hd)
            tok = windowed_gqa(q, k, v, sink_a[j], t5_table)
        else:
            j = i // N_MIXERS
            proj = jnp.einsum('bsd,de->bse', h, w_in_b[j])
            c1 = B_HEADS * hd
            q = proj[..., :c1].reshape(B, S, B_HEADS, hd)
            k = proj[..., c1:2 * c1].reshape(B, S, B_HEADS, hd)
            v = proj[..., 2 * c1:3 * c1].reshape(B, S, B_HEADS, hd)
            qm = proj[..., 3 * c1:].reshape(B, S, MEM_HEADS, hd)
            tok = neighbourhood_attn(q, k, v, rpb_b[j])
        mem_out = memory_attn(qm, mk, mv)
        mixed = jnp.einsum('bse,ed->bsd', jnp.concatenate([tok, mem_out], axis=-1), w_out[i])
        x = x + rms_norm(mixed, norm_mix_post[i])
        h = rms_norm(x, norm_ffn_pre[i])
        gu = jnp.einsum('bsd,df->bsf', h, w_gu[i])
        f = jnp.einsum('bsf,fd->bsd', jax.nn.silu(gu[..., :D_FF]) * gu[..., D_FF:], w_down[i])
        x = x + rms_norm(f, norm_ffn_post[i])
    return x
```

```python
import math
import numpy as np
import ml_dtypes
import concourse.bass as bass
import concourse.mybir as mybir
from concourse.bass_utils import run_bass_kernel_spmd

F32 = mybir.dt.float32
BF16 = mybir.dt.bfloat16
AF = mybir.ActivationFunctionType
ALU = mybir.AluOpType

HB = [0, 2, 4, 6, 8, 10, 1, 3, 5, 7, 9, 11]
D = 1024
HD = 64
DFF = 2816
MEM_LEN = 256
EPS = 1e-6
NEG = -30000.0
N_CORES = 8

ENGS = ("pe", "act", "dve", "pool", "sp")
SAME_ENG_SYNC = True
FFN_ON = True
VERBOSE = False
DEBUG_STAGE = 99
DEBUG_SUB = 99
DBG_NOMEM = False
DBG_NOXR = False
DBG_NOTOK = False


class Buf:
    __slots__ = ("name", "w", "r")

    def __init__(self, name=""):
        self.name = name
        self.w = None
        self.r = {}


class Op:
    __slots__ = ("eng", "fn", "deps", "sig", "semref", "val", "is_dma")


class Prog:
    def __init__(self, nc):
        self.nc = nc
        self.streams = {e: [] for e in ENGS}
        self.cur = {}
        self.nsem = 0
        self.dmas = []
        self.dsems = {}
        self.new_epoch()

    def new_sem(self, name):
        self.nsem += 1
        return [self.nc.alloc_semaphore(f"{name}_{self.nsem}"), 0]

    def dsem(self, name):
        if name not in self.dsems:
            self.dsems[name] = self.new_sem(name)
        return self.dsems[name]

    def new_epoch(self):
        for e in ENGS:
            if e == "sp":
                self.cur[e] = None
            else:
                self.cur[e] = self.new_sem("e_" + e)

    def op(self, e, fn, reads=(), writes=(), dsem=None):
        o = Op()
        o.eng = e
        o.fn = fn
        o.sig = False
        o.val = None
        o.is_dma = dsem is not None
        if o.is_dma:
            o.semref = dsem
            dsem[1] += 16
            o.val = dsem[1]
            o.sig = True
        else:
            o.semref = self.cur[e]
        deps = []

        def add(d, raw):
            if d is None or d is o:
                return
            if d.is_dma:
                deps.append(d)
                return
            if d.eng == e and not o.is_dma:
                if raw and e != "pe" and SAME_ENG_SYNC:
                    d.sig = True
                    deps.append(d)
                return
            d.sig = True
            deps.append(d)

        for b in reads:
            add(b.w, True)
        for b in writes:
            add(b.w, False)
            for r in b.r.values():
                add(r, False)
        for b in reads:
            b.r[("d", id(o)) if o.is_dma else e] = o
        for b in writes:
            b.w = o
            b.r = {}
        o.deps = deps
        self.streams[e].append(o)
        if o.is_dma:
            self.dmas.append(o)
        return o

    def barrier(self):
        lasts = [self.streams[e][-1] for e in ENGS if self.streams[e]]
        dmas = self.dmas
        self.dmas = []
        for e in ENGS:
            o = Op()
            o.eng = e
            o.fn = None
            o.sig = False
            o.val = None
            o.is_dma = False
            o.semref = self.cur[e]
            deps = []
            for d in lasts:
                if d.is_dma or d.fn is None:
                    continue
                if d.eng != e:
                    d.sig = True
                    deps.append(d)
            deps.extend(dmas)
            o.deps = deps
            self.streams[e].append(o)

    def emit(self, final_waits):
        nc = self.nc
        for e in ENGS:
            for o in self.streams[e]:
                if o.is_dma or not o.sig:
                    continue
                o.semref[1] += 1
                o.val = o.semref[1]
        stats = {}

        def run(e, eng):
            seen = {}
            nw = 0
            for o in self.streams[e]:
                for d in o.deps:
                    k = id(d.semref)
                    if seen.get(k, 0) >= d.val:
                        continue
                    eng.wait_ge(d.semref[0], d.val)
                    seen[k] = d.val
                    nw += 1
                if o.fn is None:
                    continue
                ins = o.fn(eng)
                if o.sig:
                    ins.then_inc(o.semref[0], 16 if o.is_dma else 1)
            if e == "sp":
                for d in final_waits:
                    eng.wait_ge(d.semref[0], d.val)
            stats[e] = (len(self.streams[e]), nw)

        with nc.Block() as block:
            @block.tensor
            def _(eng):
                run("pe", eng)

            @block.scalar
            def _(eng):
                run("act", eng)

            @block.vector
            def _(eng):
                run("dve", eng)

            @block.gpsimd
            def _(eng):
                run("pool", eng)

            @block.sync
            def _(eng):
                run("sp", eng)
        return stats


class Arena:
    def __init__(self, nc, nbytes):
        self.t = nc.alloc_sbuf_tensor("arena", [128, nbytes // 2], BF16)
        self.nbytes = nbytes
        self.off = 0
        self.hi = 0

    def mark(self):
        return self.off

    def reset(self, m):
        self.off = m

    def alloc(self, nelem, dtype):
        sz = 2 if dtype == BF16 else 4
        nb = (nelem * sz + 63) // 64 * 64
        assert self.off + nb <= self.nbytes, f"SBUF arena overflow {self.off}+{nb}>{self.nbytes}"
        ap = self.t[:, self.off // 2:(self.off + nelem * sz) // 2]
        self.off += nb
        self.hi = max(self.hi, self.off)
        if dtype != BF16:
            ap = ap.bitcast(dtype)
        return ap


def r3(ap, b):
    return ap.rearrange("p (a b) -> p a b", b=b)


def bcast_rows(dram_ap_row, n):
    return bass.AP(dram_ap_row.tensor, dram_ap_row.offset, [[0, 128], [1, n]])


def build_program(S, layer_types, n_a, n_b, nvarB, varB_of_tile, ktsB_of_tile):
    L = len(layer_types)
    T = S // 128
    NS = S // 512
    assert S % 512 == 0
    nc = bass.Bass("TRN2", target_bir_lowering=False)

    def din(name, shape, dt=F32):
        return nc.dram_tensor(name, list(shape), dt, kind="ExternalInput").ap()

    x_d = din("x", [S, D])
    mem_d = din("mem", [MEM_LEN, D])
    w_in_a = din("w_in_a", [max(n_a, 1), D, 1536])
    w_in_b = din("w_in_b", [max(n_b, 1), D, 2560])
    w_mem = din("w_mem_kv", [L, D, 512])
    w_out = din("w_out", [L, D, D])
    w_gu = din("w_gu", [L, D, 2 * DFF])
    w_dn = din("w_down", [L, DFF, D])
    gcols_d = din("gcols", [128, L * 3 * 8])
    gpost_d = din("gpost", [L * 2, D])
    sink_d = din("sink", [max(n_a, 1), 12])
    biasA_d = din("biasA", [128, 3 * 4 * 3 * 128])
    biasB_d = din("biasB", [max(n_b, 1) * nvarB, 128, 12 * 5 * 128])
    ident_d = din("ident", [128, 128], BF16)
    out_d = nc.dram_tensor("out", [S, D], F32, kind="ExternalOutput").ap()

    P = Prog(nc)
    AR = Arena(nc, 206 * 1024)
    PSbig = nc.alloc_psum_tensor("psbig", [128, 7 * 512], F32)[:]
    PS = [PSbig[:, i * 512:(i + 1) * 512] for i in range(7)]
    TPt = nc.alloc_psum_tensor("tp", [128, 1024], BF16)[:]
    PSB = [Buf(f"ps{i}") for i in range(7)]
    TPB = Buf("tp")

    ident = AR.alloc(128, BF16)
    gcols = AR.alloc(L * 3 * 8, F32)
    B_ident, B_gcols = Buf(), Buf()
    P.op("sp", lambda e: e.dma_start(out=ident, in_=ident_d), writes=[B_ident], dsem=P.dsem("c0"))
    P.op("sp", lambda e: e.dma_start(out=gcols, in_=gcols_d), writes=[B_gcols], dsem=P.dsem("c1"))
    epsb = AR.alloc(1, F32)
    B_epsb = Buf()
    P.op("pool", lambda e: e.memset(epsb, EPS), writes=[B_epsb])
    base_mark = AR.mark()

    XD = [Buf(f"xd{t}") for t in range(T)]

    def gcol(l, kind):
        i = (l * 3 + kind) * 8
        return gcols[:, i:i + 8]

    def rstd_from(ssq_ap, B_ssq, v_ap, B_v, rstd_ap, B_rstd, n):
        P.op("act", lambda e: e.activation(out=v_ap[:, 0:n], in_=ssq_ap, func=AF.Ln, scale=1.0 / D, bias=epsb),
             reads=[B_ssq, B_epsb], writes=[B_v])
        P.op("act", lambda e: e.activation(out=rstd_ap[:, 0:n], in_=v_ap[:, 0:n], func=AF.Exp, scale=-0.5),
             reads=[B_v], writes=[B_rstd])

    def load_w(dst3, src2d, nchunk, Bw, wsem, group=1, last=False):
        for c in range(0, nchunk, group):
            g = min(group, nchunk - c)
            src = src2d[c * 128:(c + g) * 128, :].rearrange("(c p) n -> p c n", p=128)
            dst = dst3[:, c:c + g, :]
            fin = last and (c + g >= nchunk)
            P.op("pool", lambda e, dst=dst, src=src: e.dma_start(out=dst, in_=src),
                 writes=[Bw] if fin else [], dsem=wsem)

    def attn_phase(l, typ, j, xsrc_d, first):
        P.new_epoch()
        AR.reset(base_mark)
        isA = typ == "A"
        NIN = 1536 if isA else 2560
        NQC = 6
        NKC = 2 if isA else 6
        NKV = 4 if isA else 12
        if isA:
            qcol0, kcol0, vcol0, mcol0 = 0, 768, 1024, 1280
        else:
            qcol0, kcol0, vcol0, mcol0 = 0, 768, 1536, 2304
        wsem = P.dsem("w")
        WIN = r3(AR.alloc(8 * NIN, BF16), NIN)
        WOUT = r3(AR.alloc(8 * D, BF16), D)
        B_W = Buf()
        B_WIN = B_WOUT = B_WMEM = B_W
        w_in = (w_in_a if isA else w_in_b)[j]
        WMEM = r3(AR.alloc(8 * 512, BF16), 512)
        load_w(WMEM, w_mem[l], 8, B_WMEM, wsem, group=4)
        load_w(WIN, w_in, 8, B_WIN, wsem, group=1)
        load_w(WOUT, w_out[l], 8, B_WOUT, wsem, group=2, last=True)
        gpost = AR.alloc(D, F32)
        B_gpost = Buf()
        P.op("sp", lambda e: e.dma_start(out=gpost, in_=bcast_rows(gpost_d[l * 2:l * 2 + 1, :], D)),
             writes=[B_gpost], dsem=P.dsem("g"))
        esink = AR.alloc(16, F32)
        B_esink = Buf()
        P.op("pool", lambda e: e.memset(esink, 0.0), writes=[B_esink])
        if isA:
            sraw = AR.alloc(16, F32)
            B_sraw = Buf()
            P.op("sp", lambda e: e.dma_start(out=sraw[:, 0:12], in_=bcast_rows(sink_d[j:j + 1, :], 12)),
                 writes=[B_sraw], dsem=P.dsem("sr"))
            P.op("act", lambda e: e.activation(out=esink[:, 0:12], in_=sraw[:, 0:12], func=AF.Exp),
                 reads=[B_sraw], writes=[B_esink])
        if isA:
            NBIAS = 3 * 4 * 3 * 128
            bias = AR.alloc(NBIAS, F32)
            B_bias = Buf()
            P.op("sp", lambda e: e.dma_start(out=bias, in_=biasA_d), writes=[B_bias], dsem=P.dsem("bias"))
        else:
            NBIAS = 12 * 5 * 128
            bias = AR.alloc(NBIAS, F32)
            B_bias = Buf()
            bias_state = {"var": None}

            def ensure_bias(var):
                if bias_state["var"] == var:
                    return
                bias_state["var"] = var
                src = biasB_d[j * nvarB + var]
                P.op("sp", lambda e, src=src: e.dma_start(out=bias, in_=src), writes=[B_bias], dsem=P.dsem("bias"))
        NXP = 2
        XP = [AR.alloc(D, F32) for _ in range(NXP)]
        B_XP = [Buf() for _ in range(NXP)]
        xpsem = [P.dsem(f"xp{i}") for i in range(NXP)]
        XN = [AR.alloc(D, BF16) for _ in range(2)]
        B_XN = [Buf() for _ in range(2)]
        junk = AR.alloc(D, BF16)
        B_junk = Buf()
        small = AR.alloc(64, F32)
        ssq_p = [small[:, 0:1], small[:, 1:2]]
        v_p = [small[:, 2:3], small[:, 3:4]]
        rs_p = [small[:, 4:5], small[:, 5:6]]
        B_ssq_p = [Buf(), Buf()]
        B_v_p = [Buf(), Buf()]
        B_rs_p = [Buf(), Buf()]
        ssq_o = [small[:, 8:10], small[:, 10:12]]
        v_o = [small[:, 12:15], small[:, 16:19]]
        rs_o = [small[:, 20:21], small[:, 21:22]]
        B_ssq_o = [Buf(), Buf()]
        B_v_o = [Buf(), Buf()]
        B_rs_o = [Buf(), Buf()]
        dsum = [small[:, 24:40], small[:, 40:56]]
        B_dsum = [Buf(), Buf()]
        rcp = [AR.alloc(16, F32), AR.alloc(16, F32)]
        B_rcp = [Buf(), Buf()]

        HT = r3(AR.alloc(8 * 512, BF16), 512)
        B_HT = [Buf() for _ in range(4)]
        NQS = 2
        QT = [r3(AR.alloc(NQC * 512, BF16), 512) for _ in range(NQS)]
        QM = [r3(AR.alloc(2 * 512, BF16), 512) for _ in range(NQS)]
        B_QT = [Buf() for _ in range(NQS)]
        B_QM = [Buf() for _ in range(NQS)]
        NKS = 3
        KT = [r3(AR.alloc(NKC * 512, BF16), 512) for _ in range(NKS)]
        B_KT = [Buf() for _ in range(NKS)]
        VT = [[r3(AR.alloc(NKV * 65, BF16), 65) for _ in range(4)] for _ in range(NKS)]
        B_VT = [[Buf() for _ in range(4)] for _ in range(NKS)]
        for ks in range(NKS):
            for i in range(4):
                P.op("pool", lambda e, ap=VT[ks][i]: e.memset(ap, 1.0), writes=[B_VT[ks][i]])
        NPT = 5
        PT = [AR.alloc(512, BF16) for _ in range(NPT)]
        B_PT = [Buf() for _ in range(NPT)]
        CAT = [AR.alloc(D, BF16) for _ in range(2)]
        B_CAT = [Buf(), Buf()]
        CATT = [r3(AR.alloc(D, BF16), 128) for _ in range(2)]
        B_CATT = [Buf(), Buf()]
        YSB = AR.alloc(D, F32)
        B_YSB = Buf()
        MKT = r3(AR.alloc(2 * 256, BF16), 256)
        B_MKT = Buf()
        MV = [r3(AR.alloc(4 * 65, BF16), 65) for _ in range(2)]
        B_MV = Buf()
        for mt in range(2):
            P.op("pool", lambda e, ap=MV[mt]: e.memset(ap, 1.0), writes=[B_MV])
        XR = [AR.alloc(D, F32) for _ in range(2)]
        B_XR = [Buf(), Buf()]

        S_ring = [(PS[0], PSB[0]), (PS[1], PSB[1])]
        O_banks = [(PS[2], PSB[2]), (PS[3], PSB[3]), (PS[4], PSB[4])]
        YP = [(PS[5], PSB[5]), (PS[6], PSB[6])]
        YPfull = PSbig[:, 5 * 512:7 * 512]
        TP3 = r3(TPt, 128)

        def o_ap(h, lo, hi):
            b, i = divmod(h, 7)
            return O_banks[b][0][:, i * 65 + lo:i * 65 + hi], O_banks[b][1]

        cnt = {"pt": 0, "yp": 0, "q": 0}

        def norm_transpose(src_rows_ap, src_bufs, dst3, col0, B_dst, gkind):
            k = cnt["pt"]
            cnt["pt"] += 1
            s2 = k % 2
            sl = k % NXP
            P.op("sp", lambda e: e.dma_start(out=XP[sl], in_=src_rows_ap), reads=src_bufs,
                 writes=[B_XP[sl]], dsem=xpsem[sl])
            P.op("act", lambda e: e.activation(out=junk, in_=XP[sl], func=AF.Square, accum_out=ssq_p[s2]),
                 reads=[B_XP[sl]], writes=[B_junk, B_ssq_p[s2]])
            rstd_from(ssq_p[s2], B_ssq_p[s2], v_p[s2], B_v_p[s2], rs_p[s2], B_rs_p[s2], 1)
            P.op("pool", lambda e: e.tensor_scalar(out=XN[s2], in0=XP[sl], scalar1=rs_p[s2], scalar2=None,
                                                   op0=ALU.mult),
                 reads=[B_XP[sl], B_rs_p[s2]], writes=[B_XN[s2]])
            for c in range(8):
                P.op("pe", lambda e, c=c: e.transpose(out=TP3[:, c, :], in_=XN[s2][:, c * 128:(c + 1) * 128],
                                                      identity=ident),
                     reads=[B_XN[s2], B_ident], writes=[TPB])
            gb = gcol(l, gkind).unsqueeze(2).to_broadcast([128, 8, 128])
            P.op("dve", lambda e: e.tensor_tensor(out=dst3[:, :, col0:col0 + 128], in0=TP3, in1=gb, op=ALU.mult),
                 reads=[TPB, B_gcols], writes=[B_dst])

        def next_yp():
            k = cnt["yp"]
            cnt["yp"] += 1
            return YP[k % 2]

        MT = HT
        for mt in range(2):
            norm_transpose(mem_d[mt * 128:(mt + 1) * 128, :], [], MT, mt * 128, B_HT[mt], 1)
        for mc in range(2):
            yp, B_yp = next_yp()
            for c in range(8):
                P.op("pe", lambda e, c=c, mc=mc, yp=yp: e.matmul(yp[:, 0:256], lhsT=WMEM[:, c, mc * 128:(mc + 1) * 128],
                                                                 rhs=MT[:, c, 0:256], start=(c == 0), stop=(c == 7)),
                     reads=[B_WMEM, B_HT[0], B_HT[1]], writes=[B_yp])
            P.op("act", lambda e, mc=mc, yp=yp: e.activation(out=MKT[:, mc, :], in_=yp[:, 0:256], func=AF.Copy),
                 reads=[B_yp], writes=[B_MKT])
        for mt in range(2):
            yp, B_yp = next_yp()
            for c in range(8):
                P.op("pe", lambda e, c=c, mt=mt, yp=yp: e.matmul(yp[:, 0:256], lhsT=MT[:, c, mt * 128:(mt + 1) * 128],
                                                                 rhs=WMEM[:, c, 256:512], start=(c == 0), stop=(c == 7)),
                     reads=[B_WMEM, B_HT[mt]], writes=[B_yp])
            P.op("dve", lambda e, mt=mt, yp=yp: e.tensor_copy(out=MV[mt][:, :, 0:64], in_=r3(yp[:, 0:256], 64)),
                 reads=[B_yp], writes=[B_MV])
        if DEBUG_STAGE <= 2:
            P.barrier()
            return
        NXR = 2
        xrsem = [P.dsem(f"xr{i}") for i in range(NXR)]
        xssem = [P.dsem(f"xs{i}") for i in range(NXR)]

        def ptile(t):
            i = t % 4
            ks = (t // 4) % NKS
            norm_transpose(xsrc_d[t * 128:(t + 1) * 128, :], [XD[t]] if not first else [], HT, i * 128, B_HT[i], 0)
            if isA:
                groups = [(0, 4)]
            else:
                groups = [(0, 6), (6, 12)]
            for (h0, h1) in groups:
                yp, B_yp = next_yp()
                n = (h1 - h0) * 64
                for c in range(8):
                    P.op("pe", lambda e, c=c, yp=yp, n=n, h0=h0: e.matmul(
                        yp[:, 0:n], lhsT=HT[:, c, i * 128:(i + 1) * 128],
                        rhs=WIN[:, c, vcol0 + h0 * 64:vcol0 + h0 * 64 + n], start=(c == 0), stop=(c == 7)),
                        reads=[B_WIN, B_HT[i]], writes=[B_yp])
                P.op("dve", lambda e, yp=yp, n=n, h0=h0, h1=h1: e.tensor_copy(
                    out=VT[ks][i][:, h0:h1, 0:64], in_=r3(yp[:, 0:n], 64)),
                    reads=[B_yp], writes=[B_VT[ks][i]])

        def chunk(dst, B_dst, lhsT_of_c, scale, eng):
            yp, B_yp = next_yp()
            for c in range(8):
                P.op("pe", lambda e, c=c, yp=yp: e.matmul(yp, lhsT=lhsT_of_c(c), rhs=HT[:, c, :],
                                                         start=(c == 0), stop=(c == 7)),
                     reads=[B_WIN] + B_HT, writes=[B_yp])
            if eng == "act":
                P.op("act", lambda e, yp=yp: e.activation(out=dst, in_=yp, func=AF.Copy, scale=scale),
                     reads=[B_yp], writes=[B_dst])
            else:
                P.op("dve", lambda e, yp=yp: e.tensor_scalar(out=dst, in0=yp, scalar1=scale, scalar2=None,
                                                             op0=ALU.mult),
                     reads=[B_yp], writes=[B_dst])

        def kchunks(s):
            ks = s % NKS
            for kc in range(NKC):
                chunk(KT[ks][:, kc, :], B_KT[ks],
                      lambda c, kc=kc: WIN[:, c, kcol0 + kc * 128:kcol0 + (kc + 1) * 128], 1.0, "dve")

        QH = [0, 1, 2, 6, 7, 8]

        def qchunks(s):
            qs = s % NQS
            for qc in range(NQC):
                def lf(c, qc=qc):
                    return WIN[:, c, qcol0 + qc * 128:qcol0 + (qc + 1) * 128]
                chunk(QT[qs][:, qc, :], B_QT[qs], lf, 0.125, "act")
            for mc in range(2):
                chunk(QM[qs][:, mc, :], B_QM[qs],
                      lambda c, mc=mc: WIN[:, c, mcol0 + mc * 128:mcol0 + (mc + 1) * 128], 0.125, "act")

        def attn(qt):
            s = qt // 4
            qi = qt % 4
            qs = s % NQS
            k2 = cnt["q"] % 2
            cnt["q"] += 1
            qcols = slice(qi * 128, (qi + 1) * 128)
            xr = qt % NXR
            if not DBG_NOXR:
              P.op("sp", lambda e: e.dma_start(out=XR[xr], in_=xsrc_d[qt * 128:(qt + 1) * 128, :]),
                 reads=[XD[qt]] if not first else [], writes=[B_XR[xr]], dsem=xrsem[xr])
            units = []
            if DBG_NOTOK:
                pass
            elif isA:
                kts = [kt for kt in (qt - 1, qt, qt + 1) if 0 <= kt < T]
                for kv in range(4):
                    off = 64 * (kv % 2)
                    c0 = 0 if kv < 2 else 3
                    for kt in kts:
                        jj = kt - qt + 1
                        ks = (kt // 4) % NKS
                        ki = kt % 4
                        units.append(dict(
                            N=384,
                            lhsT=KT[ks][off:off + 64, kv // 2, ki * 128:(ki + 1) * 128],
                            rhs=QT[qs][off:off + 64, c0:c0 + 3, qcols],
                            bias=True, bcol=((jj * 4 + kv) * 3) * 128, off=off,
                            subs=[(kv * 3 + g, g * 128) for g in range(3)],
                            v=VT[ks][ki][:, kv, :], reads=[B_KT[ks], B_QT[qs]], vreads=[B_VT[ks][ki]]))
            else:
                kts = ktsB_of_tile[qt]
                U = len(kts)
                ensure_bias(varB_of_tile[qt])
                for pos, h in enumerate(HB):
                    off = 64 * (h % 2)
                    for u, kt in enumerate(kts):
                        ks = (kt // 4) % NKS
                        ki = kt % 4
                        units.append(dict(
                            N=128,
                            lhsT=KT[ks][off:off + 64, h // 2, ki * 128:(ki + 1) * 128],
                            rhs=QT[qs][off:off + 64, h // 2, qcols],
                            bias=True, bcol=(pos * U + u) * 128, off=off,
                            subs=[(h, 0)],
                            v=VT[ks][ki][:, h, :], reads=[B_KT[ks], B_QT[qs]], vreads=[B_VT[ks][ki]]))
            for mh in ([0, 2, 1, 3] if not DBG_NOMEM else []):
                off = 64 * (mh % 2)
                for mt in range(2):
                    units.append(dict(
                        N=128,
                        lhsT=MKT[off:off + 64, mh // 2, mt * 128:(mt + 1) * 128],
                        rhs=QM[qs][off:off + 64, mh // 2, qcols],
                        bias=None, off=off, subs=[(12 + mh, 0)],
                        v=MV[mt][:, mh, :], reads=[B_MKT, B_QM[qs]], vreads=[B_MV]))
            banks = []
            cur = []
            curn = 0
            for u in units:
                hasb = u["bias"] is not None
                if cur and (curn + u["N"] > 512 or (cur[0]["bias"] is not None) != hasb or cur[0]["off"] != u["off"]):
                    banks.append(cur)
                    cur = []
                    curn = 0
                u["col"] = curn
                cur.append(u)
                curn += u["N"]
            if cur:
                banks.append(cur)
            head_units = {}
            for bi, bk in enumerate(banks):
                for u in bk:
                    u["bank"] = bi
                    for (h, co) in u["subs"]:
                        head_units.setdefault(h, []).append((u, co))
            head_last_bank = {h: max(u["bank"] for (u, _) in lst) for h, lst in head_units.items()}
            gb = cnt.setdefault("bank", 0)

            def emit_pv(bi):
                for h in sorted(head_units):
                    if head_last_bank[h] != bi:
                        continue
                    lst = head_units[h]
                    oap, B_o = o_ap(h, 0, 65)
                    for n, (u, co) in enumerate(lst):
                        pslot = (gb + u["bank"]) % NPT
                        c0 = u["col"] + co
                        P.op("pe", lambda e, u=u, pslot=pslot, c0=c0, n=n, oap=oap, last=(n == len(lst) - 1):
                             e.matmul(oap, lhsT=PT[pslot][:, c0:c0 + 128], rhs=u["v"], start=(n == 0), stop=last),
                             reads=[B_PT[pslot]] + u["vreads"], writes=[B_o])

            for bi, bk in enumerate(banks):
                sb, B_sb = S_ring[(gb + bi) % 2]
                ncols = sum(u["N"] for u in bk)
                for u in bk:
                    P.op("pe", lambda e, u=u, sb=sb: e.matmul(sb[:, u["col"]:u["col"] + u["N"]], lhsT=u["lhsT"],
                                                             rhs=u["rhs"], start=True, stop=True),
                         reads=u["reads"], writes=[B_sb])
                if bk[0]["bias"] is not None:
                    bc0 = bk[0]["bcol"]
                    assert all(u["bcol"] == bc0 + u["col"] for u in bk)
                    bias_ap = bias[:, bc0:bc0 + ncols]
                    P.op("dve", lambda e, sb=sb, ncols=ncols, bias_ap=bias_ap: e.tensor_tensor(
                        out=sb[:, 0:ncols], in0=sb[:, 0:ncols], in1=bias_ap, op=ALU.add),
                        reads=[B_sb, B_bias], writes=[B_sb])
                pslot = (gb + bi) % NPT
                P.op("act", lambda e, sb=sb, ncols=ncols, pslot=pslot: e.activation(
                    out=PT[pslot][:, 0:ncols], in_=sb[:, 0:ncols], func=AF.Exp),
                    reads=[B_sb], writes=[B_PT[pslot]])
                if bi >= 1 and DEBUG_SUB >= 2:
                    emit_pv(bi - 1)
            if DEBUG_SUB >= 2:
                emit_pv(len(banks) - 1)
            cnt["bank"] = gb + len(banks)
            if DEBUG_SUB < 3:
                return
            ds, B_ds = dsum[k2], B_dsum[k2]
            rc, B_rc = rcp[k2], B_rcp[k2]
            for b, (h0, h1) in enumerate([(0, 7), (7, 14), (14, 16)]):
                n = h1 - h0
                ob3 = r3(O_banks[b][0][:, 0:n * 65], 65)
                P.op("dve", lambda e, ob3=ob3, n=n, h0=h0, h1=h1: e.tensor_tensor(
                    out=ds[:, h0:h1], in0=ob3[:, :, 64], in1=esink[:, h0:h1], op=ALU.add),
                    reads=[O_banks[b][1], B_esink], writes=[B_ds])
            P.op("dve", lambda e: e.reciprocal(out=rc, in_=ds), reads=[B_ds], writes=[B_rc])
            cat, B_cat = CAT[k2], B_CAT[k2]
            for b, (h0, h1) in enumerate([(0, 7), (7, 14), (14, 16)]):
                n = h1 - h0
                ob3 = r3(O_banks[b][0][:, 0:n * 65], 65)
                P.op("dve", lambda e, ob3=ob3, n=n, h0=h0, h1=h1: e.tensor_tensor(
                    out=r3(cat[:, h0 * 64:h1 * 64], 64), in0=ob3[:, :, 0:64],
                    in1=rc[:, h0:h1].unsqueeze(2).to_broadcast([128, n, 64]), op=ALU.mult),
                    reads=[O_banks[b][1], B_rc], writes=[B_cat])
            if DEBUG_SUB < 4:
                return
            for c in range(8):
                P.op("pe", lambda e, c=c: e.transpose(out=TP3[:, c, :], in_=cat[:, c * 128:(c + 1) * 128],
                                                      identity=ident),
                     reads=[B_cat, B_ident], writes=[TPB])
            catt, B_catt = CATT[k2], B_CATT[k2]
            P.op("act", lambda e: e.activation(out=catt, in_=TP3, func=AF.Copy), reads=[TPB], writes=[B_catt])
            for half in range(2):
                yp, B_yp = YP[half]
                for c in range(8):
                    P.op("pe", lambda e, c=c, yp=yp, half=half: e.matmul(
                        yp, lhsT=catt[:, c, :], rhs=WOUT[:, c, half * 512:(half + 1) * 512],
                        start=(c == 0), stop=(c == 7)),
                        reads=[B_catt, B_WOUT], writes=[B_yp])
            cnt["yp"] = 0
            if DEBUG_SUB < 5:
                return
            P.op("act", lambda e: e.activation(out=junk, in_=YPfull, func=AF.Square, accum_out=ssq_o[k2][:, 0:1]),
                 reads=[YP[0][1], YP[1][1]], writes=[B_junk, B_ssq_o[k2]])
            rstd_from(ssq_o[k2][:, 0:1], B_ssq_o[k2], v_o[k2], B_v_o[k2], rs_o[k2], B_rs_o[k2], 1)
            for half in range(2):
                yp, B_yp = YP[half]
                P.op("dve", lambda e, yp=yp, half=half: e.scalar_tensor_tensor(
                    out=YSB[:, half * 512:(half + 1) * 512], in0=yp, scalar=rs_o[k2],
                    in1=gpost[:, half * 512:(half + 1) * 512], op0=ALU.mult, op1=ALU.mult),
                    reads=[B_yp, B_rs_o[k2], B_gpost], writes=[B_YSB])
            P.op("pool", lambda e: e.tensor_tensor(out=XR[xr], in0=XR[xr], in1=YSB, op=ALU.add),
                 reads=[B_XR[xr], B_YSB], writes=[B_XR[xr]])
            P.op("sp", lambda e: e.dma_start(out=out_d[qt * 128:(qt + 1) * 128, :], in_=XR[xr]),
                 reads=[B_XR[xr]], writes=[XD[qt]], dsem=xssem[xr])

        for t in range(4):
            ptile(t)
        if DEBUG_STAGE <= 3:
            P.barrier()
            return
        kchunks(0)
        qchunks(0)
        if DEBUG_STAGE <= 4:
            P.barrier()
            return
        if DEBUG_STAGE <= 5:
            attn(0)
            P.barrier()
            return
        for s in range(NS):
            nxt = s + 1 < NS
            if nxt:
                ptile(4 * (s + 1) + 0)
                ptile(4 * (s + 1) + 1)
            attn(4 * s + 0)
            if nxt:
                ptile(4 * (s + 1) + 2)
                ptile(4 * (s + 1) + 3)
            attn(4 * s + 1)
            if nxt:
                kchunks(s + 1)
            attn(4 * s + 2)
            if nxt:
                qchunks(s + 1)
            attn(4 * s + 3)
        P.barrier()

    def ffn_phase(l):
        P.new_epoch()
        AR.reset(base_mark)
        FT = 256
        NIT = S // FT
        wsem = P.dsem("w")
        WGU = r3(AR.alloc(8 * 2 * DFF, BF16), 2 * DFF)
        WDN = r3(AR.alloc(22 * D, BF16), D)
        B_WGU = B_WDN = Buf()
        load_w(WGU, w_gu[l], 8, B_WGU, wsem, group=1)
        load_w(WDN, w_dn[l], 22, B_WDN, wsem, group=4, last=True)
        gpost = AR.alloc(D, F32)
        B_gpost = Buf()
        P.op("sp", lambda e: e.dma_start(out=gpost, in_=bcast_rows(gpost_d[l * 2 + 1:l * 2 + 2, :], D)),
             writes=[B_gpost], dsem=P.dsem("g"))
        NXS = 3
        XS = [r3(AR.alloc(2 * D, F32), D) for _ in range(NXS)]
        B_XS = [Buf() for _ in range(NXS)]
        xlsem = [P.dsem(f"xl{i}") for i in range(NXS)]
        xssem = [P.dsem(f"xs{i}") for i in range(NXS)]
        XN = [AR.alloc(D, BF16) for _ in range(2)]
        B_XN = [Buf(), Buf()]
        junk = AR.alloc(D, BF16)
        B_junk = Buf()
        HT = [r3(AR.alloc(8 * FT, BF16), FT) for _ in range(2)]
        B_HT = [[Buf(), Buf()] for _ in range(2)]
        ACTT = r3(AR.alloc(22 * FT, BF16), FT)
        B_ACTT = [Buf(), Buf()]
        TH = [AR.alloc(FT, F32) for _ in range(2)]
        B_TH = [Buf(), Buf()]
        A1 = [AR.alloc(FT, F32) for _ in range(2)]
        B_A1 = [Buf(), Buf()]
        YSB = AR.alloc(D, F32)
        B_YSB = Buf()
        small = AR.alloc(64, F32)
        ssq_p = [small[:, 0:2], small[:, 2:4]]
        v_p = [small[:, 4:6], small[:, 6:8]]
        rs_p = [small[:, 8:10], small[:, 10:12]]
        B_ssq_p, B_v_p, B_rs_p = [Buf(), Buf()], [Buf(), Buf()], [Buf(), Buf()]
        ssq_o = [small[:, 16:18], small[:, 18:20]]
        v_o = [small[:, 20:23], small[:, 24:27]]
        rs_o = [small[:, 28:29], small[:, 29:30]]
        B_ssq_o, B_v_o, B_rs_o = [Buf(), Buf()], [Buf(), Buf()], [Buf(), Buf()]
        TP3 = r3(TPt, 128)
        GU = [(PS[0], PSB[0]), (PS[1], PSB[1]), (PS[2], PSB[2])]
        Y = [[(PS[3], PSB[3]), (PS[4], PSB[4])], [(PS[5], PSB[5]), (PS[6], PSB[6])]]
        gb = gcol(l, 2).unsqueeze(2).to_broadcast([128, 8, 128])
        nfc = 0
        for it in range(NIT):
            sl = it % NXS
            k2 = it % 2
            t0 = it * 2
            src = out_d[it * FT:(it + 1) * FT, :].rearrange("(t p) d -> p t d", p=128)
            P.op("sp", lambda e, sl=sl, src=src: e.dma_start(out=XS[sl], in_=src),
                 reads=[XD[t0], XD[t0 + 1]], writes=[B_XS[sl]], dsem=xlsem[sl])
            for jx in range(2):
                P.op("act", lambda e, sl=sl, jx=jx, k2=k2: e.activation(
                    out=junk, in_=XS[sl][:, jx, :], func=AF.Square, accum_out=ssq_p[k2][:, jx:jx + 1]),
                    reads=[B_XS[sl]], writes=[B_junk, B_ssq_p[k2]])
            rstd_from(ssq_p[k2], B_ssq_p[k2], v_p[k2], B_v_p[k2], rs_p[k2], B_rs_p[k2], 2)
            ht, B_ht = HT[k2], B_HT[k2]
            for jx in range(2):
                P.op("pool", lambda e, sl=sl, jx=jx, k2=k2: e.tensor_scalar(
                    out=XN[jx], in0=XS[sl][:, jx, :], scalar1=rs_p[k2][:, jx:jx + 1], scalar2=None, op0=ALU.mult),
                    reads=[B_XS[sl], B_rs_p[k2]], writes=[B_XN[jx]])
                for c in range(8):
                    P.op("pe", lambda e, c=c, jx=jx: e.transpose(out=TP3[:, c, :],
                                                                 in_=XN[jx][:, c * 128:(c + 1) * 128],
                                                                 identity=ident),
                         reads=[B_XN[jx], B_ident], writes=[TPB])
                P.op("dve", lambda e, jx=jx, ht=ht: e.tensor_tensor(out=ht[:, :, jx * 128:(jx + 1) * 128], in0=TP3,
                                                                    in1=gb, op=ALU.mult),
                     reads=[TPB, B_gcols], writes=[B_ht[jx]])
            for fc in range(22):
                gu, B_gu = GU[nfc % 3]
                t2 = nfc % 2
                nfc += 1
                for part in range(2):
                    col = part * DFF + fc * 128
                    for c in range(8):
                        P.op("pe", lambda e, c=c, gu=gu, part=part, col=col, ht=ht: e.matmul(
                            gu[:, part * FT:(part + 1) * FT], lhsT=WGU[:, c, col:col + 128], rhs=ht[:, c, :],
                            start=(c == 0), stop=(c == 7)),
                            reads=[B_WGU, B_ht[0], B_ht[1]], writes=[B_gu])
                P.op("act", lambda e, gu=gu, t2=t2: e.activation(out=TH[t2], in_=gu[:, 0:FT], func=AF.Tanh,
                                                                 scale=0.5),
                     reads=[B_gu], writes=[B_TH[t2]])
                P.op("dve", lambda e, gu=gu, t2=t2: e.scalar_tensor_tensor(
                    out=A1[t2], in0=TH[t2], scalar=1.0, in1=gu[:, 0:FT], op0=ALU.add, op1=ALU.mult),
                    reads=[B_TH[t2], B_gu], writes=[B_A1[t2]])
                P.op("dve", lambda e, gu=gu, t2=t2, fc=fc: e.scalar_tensor_tensor(
                    out=ACTT[:, fc, :], in0=A1[t2], scalar=0.5, in1=gu[:, FT:2 * FT], op0=ALU.mult, op1=ALU.mult),
                    reads=[B_A1[t2], B_gu], writes=[B_ACTT[0], B_ACTT[1]])
            for jx in range(2):
                yb = Y[jx]
                for half in range(2):
                    yp, B_yp = yb[half]
                    for fc in range(22):
                        P.op("pe", lambda e, fc=fc, yp=yp, half=half, jx=jx: e.matmul(
                            yp, lhsT=ACTT[:, fc, jx * 128:(jx + 1) * 128],
                            rhs=WDN[:, fc, half * 512:(half + 1) * 512], start=(fc == 0), stop=(fc == 21)),
                            reads=[B_ACTT[jx], B_WDN], writes=[B_yp])
                yfull = PSbig[:, (3 + 2 * jx) * 512:(5 + 2 * jx) * 512]
                P.op("act", lambda e, yfull=yfull, jx=jx: e.activation(out=junk, in_=yfull, func=AF.Square,
                                                                       accum_out=ssq_o[jx][:, 0:1]),
                     reads=[yb[0][1], yb[1][1]], writes=[B_junk, B_ssq_o[jx]])
                rstd_from(ssq_o[jx][:, 0:1], B_ssq_o[jx], v_o[jx], B_v_o[jx], rs_o[jx], B_rs_o[jx], 1)
                for half in range(2):
                    yp, B_yp = yb[half]
                    P.op("dve", lambda e, yp=yp, half=half, jx=jx: e.scalar_tensor_tensor(
                        out=YSB[:, half * 512:(half + 1) * 512], in0=yp, scalar=rs_o[jx],
                        in1=gpost[:, half * 512:(half + 1) * 512], op0=ALU.mult, op1=ALU.mult),
                        reads=[B_yp, B_rs_o[jx], B_gpost], writes=[B_YSB])
                P.op("pool", lambda e, sl=sl, jx=jx: e.tensor_tensor(out=XS[sl][:, jx, :], in0=XS[sl][:, jx, :],
                                                                      in1=YSB, op=ALU.add),
                     reads=[B_XS[sl], B_YSB], writes=[B_XS[sl]])
            dst = out_d[it * FT:(it + 1) * FT, :].rearrange("(t p) d -> p t d", p=128)
            P.op("sp", lambda e, sl=sl, dst=dst: e.dma_start(out=dst, in_=XS[sl]),
                 reads=[B_XS[sl]], writes=[XD[t0], XD[t0 + 1]], dsem=xssem[sl])
        P.barrier()

    ia = ib = 0
    for l, typ in enumerate(layer_types):
        if typ == "A":
            attn_phase(l, "A", ia, x_d if l == 0 else out_d, l == 0)
            ia += 1
        elif typ == "B":
            attn_phase(l, "B", ib, x_d if l == 0 else out_d, l == 0)
            ib += 1
        if FFN_ON:
            ffn_phase(l)
    final = [XD[t].w for t in range(T) if XD[t].w is not None]
    stats = P.emit(final)
    return nc, stats, AR.hi


def t5_bucket_np(rel):
    half = 16
    max_exact = 8
    n = -rel
    ret = np.where(n < 0, half, 0)
    n = np.abs(n)
    nf = np.maximum(n, 1).astype(np.float32)
    large = max_exact + (np.log(nf / np.float32(max_exact)) / np.float32(math.log(128 / max_exact))
                         * np.float32(half - max_exact)).astype(np.int32)
    large = np.minimum(large, half - 1)
    return ret + np.where(n < max_exact, n, large)


def build_biasA(t5_table):
    k = np.arange(128)[:, None]
    q = np.arange(128)[None, :]
    out = np.full((128, 3, 4, 3, 128), NEG, np.float32)
    for jj in range(3):
        rel = (jj - 1) * 128 + k - q
        ok = np.abs(rel) <= 128
        bk = t5_bucket_np(rel.astype(np.int32))
        for kv in range(4):
            for g in range(3):
                h = kv * 3 + g
                vals = t5_table[bk, h]
                out[:, jj, kv, g, :] = np.where(ok, vals, np.float32(NEG))
    return np.ascontiguousarray(out.reshape(128, -1))


def natten_plan(S):
    rows = S // 64
    T = S // 128
    kr = min(8, rows)

    def rs(r):
        return int(np.clip(r - kr // 2, 0, rows - kr))
    variants = []
    var_of = []
    kts_of = []
    for t in range(T):
        r0, r1 = 2 * t, 2 * t + 1
        lo = rs(r0)
        hi = rs(r1) + kr - 1
        kts = list(range(lo // 2, hi // 2 + 1))
        key = (rs(r0) - r0, rs(r1) - r1, kts[0] - t, len(kts))
        if key not in variants:
            variants.append(key)
        var_of.append(variants.index(key))
        kts_of.append(kts)
    return variants, var_of, kts_of, kr


def build_biasB(rpb, S):
    variants, var_of, kts_of, kr = natten_plan(S)
    out = np.full((len(variants), 128, 12, 5, 128), NEG, np.float32)
    kk = np.arange(128)
    rk = kk // 64
    ck = kk % 64
    rq = kk // 64
    cq = kk % 64
    cs = np.clip(cq - 8, 0, 64 - 16)
    for vi, (o0, o1, kt0, U) in enumerate(variants):
        ro = np.where(rq == 0, o0, o1)
        for u in range(U):
            drow = (2 * (kt0 + u) + rk)[:, None] - rq[None, :]
            rel_start = ro[None, :]
            okr = (drow >= rel_start) & (drow <= rel_start + kr - 1)
            okc = (ck[:, None] >= cs[None, :]) & (ck[:, None] <= cs[None, :] + 15)
            ok = okr & okc
            ri = np.clip(drow + 7, 0, 14)
            ci = np.clip(ck[:, None] - cq[None, :] + 15, 0, 30)
            for pos, h in enumerate(HB):
                vals = rpb[h][ri, ci]
                out[vi, :, pos, u, :] = np.where(ok, vals, np.float32(NEG))
        if U < 5:
            tmp = np.full((128, 12, 5, 128), NEG, np.float32)
            flat = out[vi, :, :, :U, :].reshape(128, 12 * U * 128)
            tmp.reshape(128, -1)[:, :12 * U * 128] = flat
            out[vi] = tmp
    return np.ascontiguousarray(out.reshape(len(variants), 128, -1)), var_of, kts_of


def make_inputs(S, layer_types, x, mem, w_in_a, sink_a, w_in_b, rpb_b, t5_table, w_mem_kv, w_out, w_gu, w_down,
                norm_mix_pre, norm_mix_post, norm_mem, norm_ffn_pre, norm_ffn_post):
    L = len(layer_types)
    f = lambda a: np.ascontiguousarray(np.asarray(a, dtype=np.float32))
    gc = np.zeros((128, L * 3 * 8), np.float32)
    for l in range(L):
        for kind, g in enumerate((norm_mix_pre, norm_mem, norm_ffn_pre)):
            gc[:, (l * 3 + kind) * 8:(l * 3 + kind + 1) * 8] = f(g)[l].reshape(8, 128).T
    gp = np.zeros((L * 2, D), np.float32)
    for l in range(L):
        gp[l * 2] = f(norm_mix_post)[l]
        gp[l * 2 + 1] = f(norm_ffn_post)[l]
    n_b = sum(1 for t in layer_types if t == "B")
    rp = f(rpb_b)
    tabs = []
    var_of = kts_of = None
    for jb in range(max(n_b, 1)):
        tb, var_of, kts_of = build_biasB(rp[jb], S)
        tabs.append(tb)
    biasB = np.ascontiguousarray(np.concatenate(tabs, axis=0))
    shared = dict(
        w_in_a=np.ascontiguousarray(f(w_in_a)[:, :, QPERM]), w_in_b=f(w_in_b), w_mem_kv=f(w_mem_kv), w_out=f(w_out), w_gu=f(w_gu), w_down=f(w_down),
        gcols=gc, gpost=gp, sink=f(sink_a), biasA=build_biasA(f(t5_table)), biasB=biasB,
        ident=np.eye(128, dtype=np.float32).astype(ml_dtypes.bfloat16),
    )
    return shared, biasB.shape[0] // max(n_b, 1), var_of, kts_of


_CACHE = {}
_QH = [0, 3, 1, 4, 2, 5, 6, 9, 7, 10, 8, 11]
QPERM = np.concatenate([np.arange(h * 64, (h + 1) * 64) for h in _QH] + [np.arange(768, 1536)])


def run_module(S, layer_types, inputs, n_cores=N_CORES):
    n_a = sum(1 for t in layer_types if t == "A")
    n_b = sum(1 for t in layer_types if t == "B")
    shared, nvarB, var_of, kts_of = make_inputs(S, layer_types, **inputs)
    key = (S, tuple(layer_types))
    if key not in _CACHE:
        _CACHE[key] = build_program(S, layer_types, n_a, n_b, nvarB, var_of, kts_of)
    nc, stats, hi = _CACHE[key]
    if VERBOSE:
        print("build stats", stats, "arena bytes", hi, flush=True)
    x = np.asarray(inputs["x"], dtype=np.float32)
    mem = np.asarray(inputs["mem"], dtype=np.float32)
    in_maps = []
    for c in range(n_cores):
        m = dict(shared)
        m["x"] = np.ascontiguousarray(x[c])
        m["mem"] = np.ascontiguousarray(mem[c])
        in_maps.append(m)
    res = run_bass_kernel_spmd(nc, in_maps, core_ids=list(range(n_cores)))
    return np.stack([np.asarray(r["out"]) for r in res.results], axis=0)


def kernel(x, mem, w_in_a, sink_a, w_in_b, rpb_b, t5_table, w_mem_kv, w_out, w_gu, w_down,
           norm_mix_pre, norm_mix_post, norm_mem, norm_ffn_pre, norm_ffn_post):
    inputs = dict(x=x, mem=mem, w_in_a=w_in_a, sink_a=sink_a, w_in_b=w_in_b, rpb_b=rpb_b, t5_table=t5_table,
                  w_mem_kv=w_mem_kv, w_out=w_out, w_gu=w_gu, w_down=w_down, norm_mix_pre=norm_mix_pre,
                  norm_mix_post=norm_mix_post, norm_mem=norm_mem, norm_ffn_pre=norm_ffn_pre,
                  norm_ffn_post=norm_ffn_post)
    S = np.asarray(x).shape[1]
    out = run_module(S, ["A", "B", "A", "B"], inputs)
    return out.astype(np.float32)
```

```python
import math
import numpy as np
import ml_dtypes
import concourse.bass as bass
import concourse.mybir as mybir
from concourse.bass_utils import run_bass_kernel_spmd

F32 = mybir.dt.float32
BF16 = mybir.dt.bfloat16
AF = mybir.ActivationFunctionType
ALU = mybir.AluOpType

HB = [0, 2, 4, 6, 8, 10, 1, 3, 5, 7, 9, 11]
D = 1024
HD = 64
DFF = 2816
MEM_LEN = 256
EPS = 1e-6
NEG = -30000.0
N_CORES = 8

ENGS = ("pe", "act", "dve", "pool", "sp")
SAME_ENG_SYNC = True
FFN_ON = True
VERBOSE = False
DEBUG_STAGE = 99
DEBUG_SUB = 99
DBG_NOMEM = False
DBG_NOXR = False
DBG_NOTOK = False


class Buf:
    __slots__ = ("name", "w", "r")

    def __init__(self, name=""):
        self.name = name
        self.w = None
        self.r = {}


class Op:
    __slots__ = ("eng", "fn", "deps", "sig", "semref", "val", "is_dma")


class Prog:
    def __init__(self, nc):
        self.nc = nc
        self.streams = {e: [] for e in ENGS}
        self.cur = {}
        self.nsem = 0
        self.dmas = []
        self.dsems = {}
        self.new_epoch()

    def new_sem(self, name):
        self.nsem += 1
        return [self.nc.alloc_semaphore(f"{name}_{self.nsem}"), 0]

    def dsem(self, name):
        if name not in self.dsems:
            self.dsems[name] = self.new_sem(name)
        return self.dsems[name]

    def new_epoch(self):
        for e in ENGS:
            if e == "sp":
                self.cur[e] = None
            else:
                self.cur[e] = self.new_sem("e_" + e)

    def op(self, e, fn, reads=(), writes=(), dsem=None):
        o = Op()
        o.eng = e
        o.fn = fn
        o.sig = False
        o.val = None
        o.is_dma = dsem is not None
        if o.is_dma:
            o.semref = dsem
            dsem[1] += 16
            o.val = dsem[1]
            o.sig = True
        else:
            o.semref = self.cur[e]
        deps = []

        def add(d, raw):
            if d is None or d is o:
                return
            if d.is_dma:
                deps.append(d)
                return
            if d.eng == e and not o.is_dma:
                if raw and e != "pe" and SAME_ENG_SYNC:
                    d.sig = True
                    deps.append(d)
                return
            d.sig = True
            deps.append(d)

        for b in reads:
            add(b.w, True)
        for b in writes:
            add(b.w, False)
            for r in b.r.values():
                add(r, False)
        for b in reads:
            b.r[("d", id(o)) if o.is_dma else e] = o
        for b in writes:
            b.w = o
            b.r = {}
        o.deps = deps
        self.streams[e].append(o)
        if o.is_dma:
            self.dmas.append(o)
        return o

    def barrier(self):
        lasts = [self.streams[e][-1] for e in ENGS if self.streams[e]]
        dmas = self.dmas
        self.dmas = []
        for e in ENGS:
            o = Op()
            o.eng = e
            o.fn = None
            o.sig = False
            o.val = None
            o.is_dma = False
            o.semref = self.cur[e]
            deps = []
            for d in lasts:
                if d.is_dma or d.fn is None:
                    continue
                if d.eng != e:
                    d.sig = True
                    deps.append(d)
            deps.extend(dmas)
            o.deps = deps
            self.streams[e].append(o)

    def emit(self, final_waits):
        nc = self.nc
        for e in ENGS:
            for o in self.streams[e]:
                if o.is_dma or not o.sig:
                    continue
                o.semref[1] += 1
                o.val = o.semref[1]
        stats = {}

        def run(e, eng):
            seen = {}
            nw = 0
            for o in self.streams[e]:
                for d in o.deps:
                    k = id(d.semref)
                    if seen.get(k, 0) >= d.val:
                        continue
                    eng.wait_ge(d.semref[0], d.val)
                    seen[k] = d.val
                    nw += 1
                if o.fn is None:
                    continue
                ins = o.fn(eng)
                if o.sig:
                    ins.then_inc(o.semref[0], 16 if o.is_dma else 1)
            if e == "sp":
                for d in final_waits:
                    eng.wait_ge(d.semref[0], d.val)
            stats[e] = (len(self.streams[e]), nw)

        with nc.Block() as block:
            @block.tensor
            def _(eng):
                run("pe", eng)

            @block.scalar
            def _(eng):
                run("act", eng)

            @block.vector
            def _(eng):
                run("dve", eng)

            @block.gpsimd
            def _(eng):
                run("pool", eng)

            @block.sync
            def _(eng):
                run("sp", eng)
        return stats


class Arena:
    def __init__(self, nc, nbytes):
        self.t = nc.alloc_sbuf_tensor("arena", [128, nbytes // 2], BF16)
        self.nbytes = nbytes
        self.off = 0
        self.hi = 0

    def mark(self):
        return self.off

    def reset(self, m):
        self.off = m

    def alloc(self, nelem, dtype):
        sz = 2 if dtype == BF16 else 4
        nb = (nelem * sz + 63) // 64 * 64
        assert self.off + nb <= self.nbytes, f"SBUF arena overflow {self.off}+{nb}>{self.nbytes}"
        ap = self.t[:, self.off // 2:(self.off + nelem * sz) // 2]
        self.off += nb
        self.hi = max(self.hi, self.off)
        if dtype != BF16:
            ap = ap.bitcast(dtype)
        return ap


def r3(ap, b):
    return ap.rearrange("p (a b) -> p a b", b=b)


def bcast_rows(dram_ap_row, n):
    return bass.AP(dram_ap_row.tensor, dram_ap_row.offset, [[0, 128], [1, n]])


def build_program(S, layer_types, n_a, n_b, nvarB, varB_of_tile, ktsB_of_tile):
    L = len(layer_types)
    T = S // 128
    NS = S // 512
    assert S % 512 == 0
    nc = bass.Bass("TRN2", target_bir_lowering=False)

    def din(name, shape, dt=F32):
        return nc.dram_tensor(name, list(shape), dt, kind="ExternalInput").ap()

    x_d = din("x", [S, D])
    mem_d = din("mem", [MEM_LEN, D])
    w_in_a = din("w_in_a", [max(n_a, 1), D, 1536])
    w_in_b = din("w_in_b", [max(n_b, 1), D, 2560])
    w_mem = din("w_mem_kv", [L, D, 512])
    w_out = din("w_out", [L, D, D])
    w_gu = din("w_gu", [L, D, 2 * DFF])
    w_dn = din("w_down", [L, DFF, D])
    gcols_d = din("gcols", [128, L * 3 * 8])
    gpost_d = din("gpost", [L * 2, D])
    sink_d = din("sink", [max(n_a, 1), 12])
    biasA_d = din("biasA", [128, 3 * 4 * 3 * 128])
    biasB_d = din("biasB", [max(n_b, 1) * nvarB, 128, 12 * 5 * 128])
    ident_d = din("ident", [128, 128], BF16)
    out_d = nc.dram_tensor("out", [S, D], F32, kind="ExternalOutput").ap()

    P = Prog(nc)
    AR = Arena(nc, 206 * 1024)
    PSbig = nc.alloc_psum_tensor("psbig", [128, 7 * 512], F32)[:]
    PS = [PSbig[:, i * 512:(i + 1) * 512] for i in range(7)]
    TPt = nc.alloc_psum_tensor("tp", [128, 1024], BF16)[:]
    PSB = [Buf(f"ps{i}") for i in range(7)]
    TPB = Buf("tp")

    ident = AR.alloc(128, BF16)
    gcols = AR.alloc(L * 3 * 8, F32)
    B_ident, B_gcols = Buf(), Buf()
    P.op("sp", lambda e: e.dma_start(out=ident, in_=ident_d), writes=[B_ident], dsem=P.dsem("c0"))
    P.op("sp", lambda e: e.dma_start(out=gcols, in_=gcols_d), writes=[B_gcols], dsem=P.dsem("c1"))
    epsb = AR.alloc(1, F32)
    B_epsb = Buf()
    P.op("pool", lambda e: e.memset(epsb, EPS), writes=[B_epsb])
    base_mark = AR.mark()

    XD = [Buf(f"xd{t}") for t in range(T)]

    def gcol(l, kind):
        i = (l * 3 + kind) * 8
        return gcols[:, i:i + 8]

    def rstd_from(ssq_ap, B_ssq, v_ap, B_v, rstd_ap, B_rstd, n):
        P.op("act", lambda e: e.activation(out=v_ap[:, 0:n], in_=ssq_ap, func=AF.Ln, scale=1.0 / D, bias=epsb),
             reads=[B_ssq, B_epsb], writes=[B_v])
        P.op("act", lambda e: e.activation(out=rstd_ap[:, 0:n], in_=v_ap[:, 0:n], func=AF.Exp, scale=-0.5),
             reads=[B_v], writes=[B_rstd])

    def load_w(dst3, src2d, nchunk, Bw, wsem, group=1, last=False):
        for c in range(0, nchunk, group):
            g = min(group, nchunk - c)
            src = src2d[c * 128:(c + g) * 128, :].rearrange("(c p) n -> p c n", p=128)
            dst = dst3[:, c:c + g, :]
            fin = last and (c + g >= nchunk)
            P.op("pool", lambda e, dst=dst, src=src: e.dma_start(out=dst, in_=src),
                 writes=[Bw] if fin else [], dsem=wsem)

    def attn_phase(l, typ, j, xsrc_d, first):
        P.new_epoch()
        AR.reset(base_mark)
        isA = typ == "A"
        NIN = 1536 if isA else 2560
        NQC = 6
        NKC = 2 if isA else 6
        NKV = 4 if isA else 12
        if isA:
            qcol0, kcol0, vcol0, mcol0 = 0, 768, 1024, 1280
        else:
            qcol0, kcol0, vcol0, mcol0 = 0, 768, 1536, 2304
        wsem = P.dsem("w")
        WIN = r3(AR.alloc(8 * NIN, BF16), NIN)
        WOUT = r3(AR.alloc(8 * D, BF16), D)
        B_W = Buf()
        B_WIN = B_WOUT = B_WMEM = B_W
        w_in = (w_in_a if isA else w_in_b)[j]
        WMEM = r3(AR.alloc(8 * 512, BF16), 512)
        load_w(WMEM, w_mem[l], 8, B_WMEM, wsem, group=4)
        load_w(WIN, w_in, 8, B_WIN, wsem, group=1)
        load_w(WOUT, w_out[l], 8, B_WOUT, wsem, group=2, last=True)
        gpost = AR.alloc(D, F32)
        B_gpost = Buf()
        P.op("sp", lambda e: e.dma_start(out=gpost, in_=bcast_rows(gpost_d[l * 2:l * 2 + 1, :], D)),
             writes=[B_gpost], dsem=P.dsem("g"))
        esink = AR.alloc(16, F32)
        B_esink = Buf()
        P.op("pool", lambda e: e.memset(esink, 0.0), writes=[B_esink])
        if isA:
            sraw = AR.alloc(16, F32)
            B_sraw = Buf()
            P.op("sp", lambda e: e.dma_start(out=sraw[:, 0:12], in_=bcast_rows(sink_d[j:j + 1, :], 12)),
                 writes=[B_sraw], dsem=P.dsem("sr"))
            P.op("act", lambda e: e.activation(out=esink[:, 0:12], in_=sraw[:, 0:12], func=AF.Exp),
                 reads=[B_sraw], writes=[B_esink])
        if isA:
            NBIAS = 3 * 4 * 3 * 128
            bias = AR.alloc(NBIAS, F32)
            B_bias = Buf()
            P.op("sp", lambda e: e.dma_start(out=bias, in_=biasA_d), writes=[B_bias], dsem=P.dsem("bias"))
        else:
            NBIAS = 12 * 5 * 128
            bias = AR.alloc(NBIAS, F32)
            B_bias = Buf()
            bias_state = {"var": None}

            def ensure_bias(var):
                if bias_state["var"] == var:
                    return
                bias_state["var"] = var
                src = biasB_d[j * nvarB + var]
                P.op("sp", lambda e, src=src: e.dma_start(out=bias, in_=src), writes=[B_bias], dsem=P.dsem("bias"))
        NXP = 2
        XP = [AR.alloc(D, F32) for _ in range(NXP)]
        B_XP = [Buf() for _ in range(NXP)]
        xpsem = [P.dsem(f"xp{i}") for i in range(NXP)]
        XN = [AR.alloc(D, BF16) for _ in range(2)]
        B_XN = [Buf() for _ in range(2)]
        junk = AR.alloc(D, BF16)
        B_junk = Buf()
        small = AR.alloc(64, F32)
        ssq_p = [small[:, 0:1], small[:, 1:2]]
        v_p = [small[:, 2:3], small[:, 3:4]]
        rs_p = [small[:, 4:5], small[:, 5:6]]
        B_ssq_p = [Buf(), Buf()]
        B_v_p = [Buf(), Buf()]
        B_rs_p = [Buf(), Buf()]
        ssq_o = [small[:, 8:10], small[:, 10:12]]
        v_o = [small[:, 12:15], small[:, 16:19]]
        rs_o = [small[:, 20:21], small[:, 21:22]]
        B_ssq_o = [Buf(), Buf()]
        B_v_o = [Buf(), Buf()]
        B_rs_o = [Buf(), Buf()]
        dsum = [small[:, 24:40], small[:, 40:56]]
        B_dsum = [Buf(), Buf()]
        rcp = [AR.alloc(16, F32), AR.alloc(16, F32)]
        B_rcp = [Buf(), Buf()]

        HT = r3(AR.alloc(8 * 512, BF16), 512)
        B_HT = [Buf() for _ in range(4)]
        NQS = 2
        QT = [r3(AR.alloc(NQC * 512, BF16), 512) for _ in range(NQS)]
        QM = [r3(AR.alloc(2 * 512, BF16), 512) for _ in range(NQS)]
        B_QT = [Buf() for _ in range(NQS)]
        B_QM = [Buf() for _ in range(NQS)]
        NKS = 3
        KT = [r3(AR.alloc(NKC * 512, BF16), 512) for _ in range(NKS)]
        B_KT = [Buf() for _ in range(NKS)]
        VT = [[r3(AR.alloc(NKV * 65, BF16), 65) for _ in range(4)] for _ in range(NKS)]
        B_VT = [[Buf() for _ in range(4)] for _ in range(NKS)]
        for ks in range(NKS):
            for i in range(4):
                P.op("pool", lambda e, ap=VT[ks][i]: e.memset(ap, 1.0), writes=[B_VT[ks][i]])
        NPT = 5
        PT = [AR.alloc(512, BF16) for _ in range(NPT)]
        B_PT = [Buf() for _ in range(NPT)]
        CAT = [AR.alloc(D, BF16) for _ in range(2)]
        B_CAT = [Buf(), Buf()]
        CATT = [r3(AR.alloc(D, BF16), 128) for _ in range(2)]
        B_CATT = [Buf(), Buf()]
        YSB = AR.alloc(D, F32)
        B_YSB = Buf()
        MKT = r3(AR.alloc(2 * 256, BF16), 256)
        B_MKT = Buf()
        MV = [r3(AR.alloc(4 * 65, BF16), 65) for _ in range(2)]
        B_MV = Buf()
        for mt in range(2):
            P.op("pool", lambda e, ap=MV[mt]: e.memset(ap, 1.0), writes=[B_MV])
        XR = [AR.alloc(D, F32) for _ in range(2)]
        B_XR = [Buf(), Buf()]

        S_ring = [(PS[0], PSB[0]), (PS[1], PSB[1])]
        O_banks = [(PS[2], PSB[2]), (PS[3], PSB[3]), (PS[4], PSB[4])]
        YP = [(PS[5], PSB[5]), (PS[6], PSB[6])]
        YPfull = PSbig[:, 5 * 512:7 * 512]
        TP3 = r3(TPt, 128)

        def o_ap(h, lo, hi):
            b, i = divmod(h, 7)
            return O_banks[b][0][:, i * 65 + lo:i * 65 + hi], O_banks[b][1]

        cnt = {"pt": 0, "yp": 0, "q": 0}

        def norm_transpose(src_rows_ap, src_bufs, dst3, col0, B_dst, gkind):
            k = cnt["pt"]
            cnt["pt"] += 1
            s2 = k % 2
            sl = k % NXP
            P.op("sp", lambda e: e.dma_start(out=XP[sl], in_=src_rows_ap), reads=src_bufs,
                 writes=[B_XP[sl]], dsem=xpsem[sl])
            P.op("act", lambda e: e.activation(out=junk, in_=XP[sl], func=AF.Square, accum_out=ssq_p[s2]),
                 reads=[B_XP[sl]], writes=[B_junk, B_ssq_p[s2]])
            rstd_from(ssq_p[s2], B_ssq_p[s2], v_p[s2], B_v_p[s2], rs_p[s2], B_rs_p[s2], 1)
            P.op("dve", lambda e: e.tensor_scalar(out=XN[s2], in0=XP[sl], scalar1=rs_p[s2], scalar2=None,
                                                  op0=ALU.mult),
                 reads=[B_XP[sl], B_rs_p[s2]], writes=[B_XN[s2]])
            for c in range(8):
                P.op("pe", lambda e, c=c: e.transpose(out=TP3[:, c, :], in_=XN[s2][:, c * 128:(c + 1) * 128],
                                                      identity=ident),
                     reads=[B_XN[s2], B_ident], writes=[TPB])
            gb = gcol(l, gkind).unsqueeze(2).to_broadcast([128, 8, 128])
            P.op("dve", lambda e: e.tensor_tensor(out=dst3[:, :, col0:col0 + 128], in0=TP3, in1=gb, op=ALU.mult),
                 reads=[TPB, B_gcols], writes=[B_dst])

        def next_yp():
            k = cnt["yp"]
            cnt["yp"] += 1
            return YP[k % 2]

        MT = HT
        for mt in range(2):
            norm_transpose(mem_d[mt * 128:(mt + 1) * 128, :], [], MT, mt * 128, B_HT[mt], 1)
        for mc in range(2):
            yp, B_yp = next_yp()
            for c in range(8):
                P.op("pe", lambda e, c=c, mc=mc, yp=yp: e.matmul(yp[:, 0:256], lhsT=WMEM[:, c, mc * 128:(mc + 1) * 128],
                                                                 rhs=MT[:, c, 0:256], start=(c == 0), stop=(c == 7)),
                     reads=[B_WMEM, B_HT[0], B_HT[1]], writes=[B_yp])
            P.op("act", lambda e, mc=mc, yp=yp: e.activation(out=MKT[:, mc, :], in_=yp[:, 0:256], func=AF.Copy),
                 reads=[B_yp], writes=[B_MKT])
        for mt in range(2):
            yp, B_yp = next_yp()
            for c in range(8):
                P.op("pe", lambda e, c=c, mt=mt, yp=yp: e.matmul(yp[:, 0:256], lhsT=MT[:, c, mt * 128:(mt + 1) * 128],
                                                                 rhs=WMEM[:, c, 256:512], start=(c == 0), stop=(c == 7)),
                     reads=[B_WMEM, B_HT[mt]], writes=[B_yp])
            P.op("dve", lambda e, mt=mt, yp=yp: e.tensor_copy(out=MV[mt][:, :, 0:64], in_=r3(yp[:, 0:256], 64)),
                 reads=[B_yp], writes=[B_MV])
        if DEBUG_STAGE <= 2:
            P.barrier()
            return
        NXR = 2
        xrsem = [P.dsem(f"xr{i}") for i in range(NXR)]
        xssem = [P.dsem(f"xs{i}") for i in range(NXR)]

        def ptile(t):
            i = t % 4
            ks = (t // 4) % NKS
            norm_transpose(xsrc_d[t * 128:(t + 1) * 128, :], [XD[t]] if not first else [], HT, i * 128, B_HT[i], 0)
            if isA:
                groups = [(0, 4)]
            else:
                groups = [(0, 6), (6, 12)]
            for (h0, h1) in groups:
                yp, B_yp = next_yp()
                n = (h1 - h0) * 64
                for c in range(8):
                    P.op("pe", lambda e, c=c, yp=yp, n=n, h0=h0: e.matmul(
                        yp[:, 0:n], lhsT=HT[:, c, i * 128:(i + 1) * 128],
                        rhs=WIN[:, c, vcol0 + h0 * 64:vcol0 + h0 * 64 + n], start=(c == 0), stop=(c == 7)),
                        reads=[B_WIN, B_HT[i]], writes=[B_yp])
                P.op("dve", lambda e, yp=yp, n=n, h0=h0, h1=h1: e.tensor_copy(
                    out=VT[ks][i][:, h0:h1, 0:64], in_=r3(yp[:, 0:n], 64)),
                    reads=[B_yp], writes=[B_VT[ks][i]])

        def chunk(dst, B_dst, lhsT_of_c, scale, eng):
            yp, B_yp = next_yp()
            for c in range(8):
                P.op("pe", lambda e, c=c, yp=yp: e.matmul(yp, lhsT=lhsT_of_c(c), rhs=HT[:, c, :],
                                                         start=(c == 0), stop=(c == 7)),
                     reads=[B_WIN] + B_HT, writes=[B_yp])
            if eng == "act":
                P.op("act", lambda e, yp=yp: e.activation(out=dst, in_=yp, func=AF.Copy, scale=scale),
                     reads=[B_yp], writes=[B_dst])
            else:
                P.op("dve", lambda e, yp=yp: e.tensor_scalar(out=dst, in0=yp, scalar1=scale, scalar2=None,
                                                             op0=ALU.mult),
                     reads=[B_yp], writes=[B_dst])

        def kchunks(s):
            ks = s % NKS
            for kc in range(NKC):
                chunk(KT[ks][:, kc, :], B_KT[ks],
                      lambda c, kc=kc: WIN[:, c, kcol0 + kc * 128:kcol0 + (kc + 1) * 128], 1.0, "dve")

        QH = [0, 1, 2, 6, 7, 8]

        def qchunks(s, part=None):
            qs = s % NQS
            for qc in range(NQC):
                if part is not None and (qc < 4) != (part == 0):
                    continue
                def lf(c, qc=qc):
                    return WIN[:, c, qcol0 + qc * 128:qcol0 + (qc + 1) * 128]
                chunk(QT[qs][:, qc, :], B_QT[qs], lf, 0.125, "act")
            for mc in range(2):
                if part == 0:
                    continue
                chunk(QM[qs][:, mc, :], B_QM[qs],
                      lambda c, mc=mc: WIN[:, c, mcol0 + mc * 128:mcol0 + (mc + 1) * 128], 0.125, "act")

        def attn(qt):
            s = qt // 4
            qi = qt % 4
            qs = s % NQS
            k2 = cnt["q"] % 2
            cnt["q"] += 1
            qcols = slice(qi * 128, (qi + 1) * 128)
            xr = qt % NXR
            if not DBG_NOXR:
              P.op("sp", lambda e: e.dma_start(out=XR[xr], in_=xsrc_d[qt * 128:(qt + 1) * 128, :]),
                 reads=[XD[qt]] if not first else [], writes=[B_XR[xr]], dsem=xrsem[xr])
            units = []
            if DBG_NOTOK:
                pass
            elif isA:
                kts = [kt for kt in (qt - 1, qt, qt + 1) if 0 <= kt < T]
                for kv in range(4):
                    off = 64 * (kv % 2)
                    c0 = 0 if kv < 2 else 3
                    for kt in kts:
                        jj = kt - qt + 1
                        ks = (kt // 4) % NKS
                        ki = kt % 4
                        units.append(dict(
                            N=384,
                            lhsT=KT[ks][off:off + 64, kv // 2, ki * 128:(ki + 1) * 128],
                            rhs=QT[qs][off:off + 64, c0:c0 + 3, qcols],
                            bias=True, bcol=((jj * 4 + kv) * 3) * 128, off=off,
                            subs=[(kv * 3 + g, g * 128) for g in range(3)],
                            v=VT[ks][ki][:, kv, :], reads=[B_KT[ks], B_QT[qs]], vreads=[B_VT[ks][ki]]))
            else:
                kts = ktsB_of_tile[qt]
                U = len(kts)
                ensure_bias(varB_of_tile[qt])
                for pos, h in enumerate(HB):
                    off = 64 * (h % 2)
                    for u, kt in enumerate(kts):
                        ks = (kt // 4) % NKS
                        ki = kt % 4
                        units.append(dict(
                            N=128,
                            lhsT=KT[ks][off:off + 64, h // 2, ki * 128:(ki + 1) * 128],
                            rhs=QT[qs][off:off + 64, h // 2, qcols],
                            bias=True, bcol=(pos * U + u) * 128, off=off,
                            subs=[(h, 0)],
                            v=VT[ks][ki][:, h, :], reads=[B_KT[ks], B_QT[qs]], vreads=[B_VT[ks][ki]]))
            for mh in ([0, 2, 1, 3] if not DBG_NOMEM else []):
                off = 64 * (mh % 2)
                for mt in range(2):
                    units.append(dict(
                        N=128,
                        lhsT=MKT[off:off + 64, mh // 2, mt * 128:(mt + 1) * 128],
                        rhs=QM[qs][off:off + 64, mh // 2, qcols],
                        bias=None, off=off, subs=[(12 + mh, 0)],
                        v=MV[mt][:, mh, :], reads=[B_MKT, B_QM[qs]], vreads=[B_MV]))
            banks = []
            cur = []
            curn = 0
            for u in units:
                hasb = u["bias"] is not None
                if cur and (curn + u["N"] > 512 or (cur[0]["bias"] is not None) != hasb or cur[0]["off"] != u["off"]):
                    banks.append(cur)
                    cur = []
                    curn = 0
                u["col"] = curn
                cur.append(u)
                curn += u["N"]
            if cur:
                banks.append(cur)
            head_units = {}
            for bi, bk in enumerate(banks):
                for u in bk:
                    u["bank"] = bi
                    for (h, co) in u["subs"]:
                        head_units.setdefault(h, []).append((u, co))
            head_last_bank = {h: max(u["bank"] for (u, _) in lst) for h, lst in head_units.items()}
            gb = cnt.setdefault("bank", 0)

            def emit_pv(bi):
                for h in sorted(head_units):
                    if head_last_bank[h] != bi:
                        continue
                    lst = head_units[h]
                    oap, B_o = o_ap(h, 0, 65)
                    for n, (u, co) in enumerate(lst):
                        pslot = (gb + u["bank"]) % NPT
                        c0 = u["col"] + co
                        P.op("pe", lambda e, u=u, pslot=pslot, c0=c0, n=n, oap=oap, last=(n == len(lst) - 1):
                             e.matmul(oap, lhsT=PT[pslot][:, c0:c0 + 128], rhs=u["v"], start=(n == 0), stop=last),
                             reads=[B_PT[pslot]] + u["vreads"], writes=[B_o])

            for bi, bk in enumerate(banks):
                sb, B_sb = S_ring[(gb + bi) % 2]
                ncols = sum(u["N"] for u in bk)
                for u in bk:
                    P.op("pe", lambda e, u=u, sb=sb: e.matmul(sb[:, u["col"]:u["col"] + u["N"]], lhsT=u["lhsT"],
                                                             rhs=u["rhs"], start=True, stop=True),
                         reads=u["reads"], writes=[B_sb])
                if bk[0]["bias"] is not None:
                    bc0 = bk[0]["bcol"]
                    assert all(u["bcol"] == bc0 + u["col"] for u in bk)
                    bias_ap = bias[:, bc0:bc0 + ncols]
                    P.op("dve", lambda e, sb=sb, ncols=ncols, bias_ap=bias_ap: e.tensor_tensor(
                        out=sb[:, 0:ncols], in0=sb[:, 0:ncols], in1=bias_ap, op=ALU.add),
                        reads=[B_sb, B_bias], writes=[B_sb])
                pslot = (gb + bi) % NPT
                P.op("act", lambda e, sb=sb, ncols=ncols, pslot=pslot: e.activation(
                    out=PT[pslot][:, 0:ncols], in_=sb[:, 0:ncols], func=AF.Exp),
                    reads=[B_sb], writes=[B_PT[pslot]])
                if bi >= 1 and DEBUG_SUB >= 2:
                    emit_pv(bi - 1)
            if DEBUG_SUB >= 2:
                emit_pv(len(banks) - 1)
            cnt["bank"] = gb + len(banks)
            ds, B_ds = dsum[k2], B_dsum[k2]
            rc, B_rc = rcp[k2], B_rcp[k2]
            for b, (h0, h1) in enumerate([(0, 7), (7, 14), (14, 16)]):
                n = h1 - h0
                ob3 = r3(O_banks[b][0][:, 0:n * 65], 65)
                P.op("dve", lambda e, ob3=ob3, n=n, h0=h0, h1=h1: e.tensor_tensor(
                    out=ds[:, h0:h1], in0=ob3[:, :, 64], in1=esink[:, h0:h1], op=ALU.add),
                    reads=[O_banks[b][1], B_esink], writes=[B_ds])
            P.op("dve", lambda e: e.reciprocal(out=rc, in_=ds), reads=[B_ds], writes=[B_rc])
            cat, B_cat = CAT[k2], B_CAT[k2]
            for b, (h0, h1) in enumerate([(0, 7), (7, 14), (14, 16)]):
                n = h1 - h0
                ob3 = r3(O_banks[b][0][:, 0:n * 65], 65)
                P.op("dve", lambda e, ob3=ob3, n=n, h0=h0, h1=h1: e.tensor_tensor(
                    out=r3(cat[:, h0 * 64:h1 * 64], 64), in0=ob3[:, :, 0:64],
                    in1=rc[:, h0:h1].unsqueeze(2).to_broadcast([128, n, 64]), op=ALU.mult),
                    reads=[O_banks[b][1], B_rc], writes=[B_cat])
            yield
            for c in range(8):
                P.op("pe", lambda e, c=c: e.transpose(out=TP3[:, c, :], in_=cat[:, c * 128:(c + 1) * 128],
                                                      identity=ident),
                     reads=[B_cat, B_ident], writes=[TPB])
            catt, B_catt = CATT[k2], B_CATT[k2]
            P.op("act", lambda e: e.activation(out=catt, in_=TP3, func=AF.Copy), reads=[TPB], writes=[B_catt])
            for half in range(2):
                yp, B_yp = YP[half]
                for c in range(8):
                    P.op("pe", lambda e, c=c, yp=yp, half=half: e.matmul(
                        yp, lhsT=catt[:, c, :], rhs=WOUT[:, c, half * 512:(half + 1) * 512],
                        start=(c == 0), stop=(c == 7)),
                        reads=[B_catt, B_WOUT], writes=[B_yp])
            cnt["yp"] = 0
            P.op("act", lambda e: e.activation(out=junk, in_=YPfull, func=AF.Square, accum_out=ssq_o[k2][:, 0:1]),
                 reads=[YP[0][1], YP[1][1]], writes=[B_junk, B_ssq_o[k2]])
            rstd_from(ssq_o[k2][:, 0:1], B_ssq_o[k2], v_o[k2], B_v_o[k2], rs_o[k2], B_rs_o[k2], 1)
            for half in range(2):
                yp, B_yp = YP[half]
                P.op("dve", lambda e, yp=yp, half=half: e.scalar_tensor_tensor(
                    out=YSB[:, half * 512:(half + 1) * 512], in0=yp, scalar=rs_o[k2],
                    in1=gpost[:, half * 512:(half + 1) * 512], op0=ALU.mult, op1=ALU.mult),
                    reads=[B_yp, B_rs_o[k2], B_gpost], writes=[B_YSB])
            P.op("pool", lambda e: e.tensor_tensor(out=XR[xr], in0=XR[xr], in1=YSB, op=ALU.add),
                 reads=[B_XR[xr], B_YSB], writes=[B_XR[xr]])
            P.op("sp", lambda e: e.dma_start(out=out_d[qt * 128:(qt + 1) * 128, :], in_=XR[xr]),
                 reads=[B_XR[xr]], writes=[XD[qt]], dsem=xssem[xr])

        for t in range(4):
            ptile(t)
        if DEBUG_STAGE <= 3:
            P.barrier()
            return
        kchunks(0)
        qchunks(0)
        if DEBUG_STAGE <= 4:
            P.barrier()
            return
        def run_attn(qt, between):
            g = attn(qt)
            next(g)
            for f in between:
                f()
            for _ in g:
                pass

        for s in range(NS):
            nxt = s + 1 < NS
            t1 = 4 * (s + 1)
            run_attn(4 * s + 0, [lambda: ptile(t1 + 0), lambda: ptile(t1 + 1)] if nxt else [])
            run_attn(4 * s + 1, [lambda: ptile(t1 + 2), lambda: ptile(t1 + 3), lambda: kchunks(s + 1)] if nxt else [])
            run_attn(4 * s + 2, [lambda: qchunks(s + 1, 0)] if nxt else [])
            run_attn(4 * s + 3, [lambda: qchunks(s + 1, 1)] if nxt else [])
        P.barrier()

    def ffn_phase(l):
        P.new_epoch()
        AR.reset(base_mark)
        FT = 256
        NIT = S // FT
        wsem = P.dsem("w")
        WGU = r3(AR.alloc(8 * 2 * DFF, BF16), 2 * DFF)
        WDN = r3(AR.alloc(22 * D, BF16), D)
        B_WGU = B_WDN = Buf()
        load_w(WGU, w_gu[l], 8, B_WGU, wsem, group=1)
        load_w(WDN, w_dn[l], 22, B_WDN, wsem, group=4, last=True)
        gpost = AR.alloc(D, F32)
        B_gpost = Buf()
        P.op("sp", lambda e: e.dma_start(out=gpost, in_=bcast_rows(gpost_d[l * 2 + 1:l * 2 + 2, :], D)),
             writes=[B_gpost], dsem=P.dsem("g"))
        NXS = 3
        XS = [r3(AR.alloc(2 * D, F32), D) for _ in range(NXS)]
        B_XS = [Buf() for _ in range(NXS)]
        xlsem = [P.dsem(f"xl{i}") for i in range(NXS)]
        xssem = [P.dsem(f"xs{i}") for i in range(NXS)]
        XN = [AR.alloc(D, BF16) for _ in range(2)]
        B_XN = [Buf(), Buf()]
        junk = AR.alloc(D, BF16)
        B_junk = Buf()
        HT = [r3(AR.alloc(8 * FT, BF16), FT) for _ in range(2)]
        B_HT = [[Buf(), Buf()] for _ in range(2)]
        ACTT = r3(AR.alloc(22 * FT, BF16), FT)
        B_ACTT = [Buf(), Buf()]
        TH = [AR.alloc(FT, F32) for _ in range(2)]
        B_TH = [Buf(), Buf()]
        A1 = [AR.alloc(FT, F32) for _ in range(2)]
        B_A1 = [Buf(), Buf()]
        YSB = AR.alloc(D, F32)
        B_YSB = Buf()
        small = AR.alloc(64, F32)
        ssq_p = [small[:, 0:2], small[:, 2:4]]
        v_p = [small[:, 4:6], small[:, 6:8]]
        rs_p = [small[:, 8:10], small[:, 10:12]]
        B_ssq_p, B_v_p, B_rs_p = [Buf(), Buf()], [Buf(), Buf()], [Buf(), Buf()]
        ssq_o = [small[:, 16:18], small[:, 18:20]]
        v_o = [small[:, 20:23], small[:, 24:27]]
        rs_o = [small[:, 28:29], small[:, 29:30]]
        B_ssq_o, B_v_o, B_rs_o = [Buf(), Buf()], [Buf(), Buf()], [Buf(), Buf()]
        TP3 = r3(TPt, 128)
        GU = [(PS[0], PSB[0]), (PS[1], PSB[1]), (PS[2], PSB[2])]
        Y = [[(PS[3], PSB[3]), (PS[4], PSB[4])], [(PS[5], PSB[5]), (PS[6], PSB[6])]]
        gb = gcol(l, 2).unsqueeze(2).to_broadcast([128, 8, 128])
        nfcc = {"n": 0}

        def pre(it):
            sl = it % NXS
            k2 = it % 2
            t0 = it * 2
            src = out_d[it * FT:(it + 1) * FT, :].rearrange("(t p) d -> p t d", p=128)
            P.op("sp", lambda e, sl=sl, src=src: e.dma_start(out=XS[sl], in_=src),
                 reads=[XD[t0], XD[t0 + 1]], writes=[B_XS[sl]], dsem=xlsem[sl])
            for jx in range(2):
                P.op("act", lambda e, sl=sl, jx=jx, k2=k2: e.activation(
                    out=junk, in_=XS[sl][:, jx, :], func=AF.Square, accum_out=ssq_p[k2][:, jx:jx + 1]),
                    reads=[B_XS[sl]], writes=[B_junk, B_ssq_p[k2]])
            rstd_from(ssq_p[k2], B_ssq_p[k2], v_p[k2], B_v_p[k2], rs_p[k2], B_rs_p[k2], 2)
            ht, B_ht = HT[k2], B_HT[k2]
            for jx in range(2):
                P.op("act", lambda e, sl=sl, jx=jx, k2=k2: e.activation(
                    out=XN[jx], in_=XS[sl][:, jx, :], func=AF.Copy, scale=rs_p[k2][:, jx:jx + 1]),
                    reads=[B_XS[sl], B_rs_p[k2]], writes=[B_XN[jx]])
                for c in range(8):
                    P.op("pe", lambda e, c=c, jx=jx: e.transpose(out=TP3[:, c, :],
                                                                 in_=XN[jx][:, c * 128:(c + 1) * 128],
                                                                 identity=ident),
                         reads=[B_XN[jx], B_ident], writes=[TPB])
                P.op("dve", lambda e, jx=jx, ht=ht: e.tensor_tensor(out=ht[:, :, jx * 128:(jx + 1) * 128], in0=TP3,
                                                                    in1=gb, op=ALU.mult),
                     reads=[TPB, B_gcols], writes=[B_ht[jx]])

        pre(0)
        for it in range(NIT):
            sl = it % NXS
            k2 = it % 2
            t0 = it * 2
            ht, B_ht = HT[k2], B_HT[k2]
            if it + 1 < NIT:
                pre(it + 1)
            nfc = nfcc["n"]
            for fc in range(22):
                gu, B_gu = GU[nfc % 3]
                t2 = nfc % 2
                nfc += 1
                for part in range(2):
                    col = part * DFF + fc * 128
                    for c in range(8):
                        P.op("pe", lambda e, c=c, gu=gu, part=part, col=col, ht=ht: e.matmul(
                            gu[:, part * FT:(part + 1) * FT], lhsT=WGU[:, c, col:col + 128], rhs=ht[:, c, :],
                            start=(c == 0), stop=(c == 7)),
                            reads=[B_WGU, B_ht[0], B_ht[1]], writes=[B_gu])
                P.op("act", lambda e, gu=gu, t2=t2: e.activation(out=TH[t2], in_=gu[:, 0:FT], func=AF.Tanh,
                                                                 scale=0.5),
                     reads=[B_gu], writes=[B_TH[t2]])
                P.op("dve", lambda e, gu=gu, t2=t2: e.scalar_tensor_tensor(
                    out=A1[t2], in0=TH[t2], scalar=1.0, in1=gu[:, 0:FT], op0=ALU.add, op1=ALU.mult),
                    reads=[B_TH[t2], B_gu], writes=[B_A1[t2]])
                P.op("dve", lambda e, gu=gu, t2=t2, fc=fc: e.scalar_tensor_tensor(
                    out=ACTT[:, fc, :], in0=A1[t2], scalar=0.5, in1=gu[:, FT:2 * FT], op0=ALU.mult, op1=ALU.mult),
                    reads=[B_A1[t2], B_gu], writes=[B_ACTT[0], B_ACTT[1]])
            nfcc["n"] = nfc
            for jx in range(2):
                yb = Y[jx]
                for half in range(2):
                    yp, B_yp = yb[half]
                    for fc in range(22):
                        P.op("pe", lambda e, fc=fc, yp=yp, half=half, jx=jx: e.matmul(
                            yp, lhsT=ACTT[:, fc, jx * 128:(jx + 1) * 128],
                            rhs=WDN[:, fc, half * 512:(half + 1) * 512], start=(fc == 0), stop=(fc == 21)),
                            reads=[B_ACTT[jx], B_WDN], writes=[B_yp])
                yfull = PSbig[:, (3 + 2 * jx) * 512:(5 + 2 * jx) * 512]
                P.op("act", lambda e, yfull=yfull, jx=jx: e.activation(out=junk, in_=yfull, func=AF.Square,
                                                                       accum_out=ssq_o[jx][:, 0:1]),
                     reads=[yb[0][1], yb[1][1]], writes=[B_junk, B_ssq_o[jx]])
                rstd_from(ssq_o[jx][:, 0:1], B_ssq_o[jx], v_o[jx], B_v_o[jx], rs_o[jx], B_rs_o[jx], 1)
                for half in range(2):
                    yp, B_yp = yb[half]
                    P.op("dve", lambda e, yp=yp, half=half, jx=jx: e.scalar_tensor_tensor(
                        out=YSB[:, half * 512:(half + 1) * 512], in0=yp, scalar=rs_o[jx],
                        in1=gpost[:, half * 512:(half + 1) * 512], op0=ALU.mult, op1=ALU.mult),
                        reads=[B_yp, B_rs_o[jx], B_gpost], writes=[B_YSB])
                P.op("pool", lambda e, sl=sl, jx=jx: e.tensor_tensor(out=XS[sl][:, jx, :], in0=XS[sl][:, jx, :],
                                                                      in1=YSB, op=ALU.add),
                     reads=[B_XS[sl], B_YSB], writes=[B_XS[sl]])
            dst = out_d[it * FT:(it + 1) * FT, :].rearrange("(t p) d -> p t d", p=128)
            P.op("sp", lambda e, sl=sl, dst=dst: e.dma_start(out=dst, in_=XS[sl]),
                 reads=[B_XS[sl]], writes=[XD[t0], XD[t0 + 1]], dsem=xssem[sl])
        P.barrier()

    ia = ib = 0
    for l, typ in enumerate(layer_types):
        if typ == "A":
            attn_phase(l, "A", ia, x_d if l == 0 else out_d, l == 0)
            ia += 1
        elif typ == "B":
            attn_phase(l, "B", ib, x_d if l == 0 else out_d, l == 0)
            ib += 1
        if FFN_ON:
            ffn_phase(l)
    final = [XD[t].w for t in range(T) if XD[t].w is not None]
    stats = P.emit(final)
    return nc, stats, AR.hi


def t5_bucket_np(rel):
    half = 16
    max_exact = 8
    n = -rel
    ret = np.where(n < 0, half, 0)
    n = np.abs(n)
    nf = np.maximum(n, 1).astype(np.float32)
    large = max_exact + (np.log(nf / np.float32(max_exact)) / np.float32(math.log(128 / max_exact))
                         * np.float32(half - max_exact)).astype(np.int32)
    large = np.minimum(large, half - 1)
    return ret + np.where(n < max_exact, n, large)


def build_biasA(t5_table):
    k = np.arange(128)[:, None]
    q = np.arange(128)[None, :]
    out = np.full((128, 3, 4, 3, 128), NEG, np.float32)
    for jj in range(3):
        rel = (jj - 1) * 128 + k - q
        ok = np.abs(rel) <= 128
        bk = t5_bucket_np(rel.astype(np.int32))
        for kv in range(4):
            for g in range(3):
                h = kv * 3 + g
                vals = t5_table[bk, h]
                out[:, jj, kv, g, :] = np.where(ok, vals, np.float32(NEG))
    return np.ascontiguousarray(out.reshape(128, -1))


def natten_plan(S):
    rows = S // 64
    T = S // 128
    kr = min(8, rows)

    def rs(r):
        return int(np.clip(r - kr // 2, 0, rows - kr))
    variants = []
    var_of = []
    kts_of = []
    for t in range(T):
        r0, r1 = 2 * t, 2 * t + 1
        lo = rs(r0)
        hi = rs(r1) + kr - 1
        kts = list(range(lo // 2, hi // 2 + 1))
        key = (rs(r0) - r0, rs(r1) - r1, kts[0] - t, len(kts))
        if key not in variants:
            variants.append(key)
        var_of.append(variants.index(key))
        kts_of.append(kts)
    return variants, var_of, kts_of, kr


def build_biasB(rpb, S):
    variants, var_of, kts_of, kr = natten_plan(S)
    out = np.full((len(variants), 128, 12, 5, 128), NEG, np.float32)
    kk = np.arange(128)
    rk = kk // 64
    ck = kk % 64
    rq = kk // 64
    cq = kk % 64
    cs = np.clip(cq - 8, 0, 64 - 16)
    for vi, (o0, o1, kt0, U) in enumerate(variants):
        ro = np.where(rq == 0, o0, o1)
        for u in range(U):
            drow = (2 * (kt0 + u) + rk)[:, None] - rq[None, :]
            rel_start = ro[None, :]
            okr = (drow >= rel_start) & (drow <= rel_start + kr - 1)
            okc = (ck[:, None] >= cs[None, :]) & (ck[:, None] <= cs[None, :] + 15)
            ok = okr & okc
            ri = np.clip(drow + 7, 0, 14)
            ci = np.clip(ck[:, None] - cq[None, :] + 15, 0, 30)
            for pos, h in enumerate(HB):
                vals = rpb[h][ri, ci]
                out[vi, :, pos, u, :] = np.where(ok, vals, np.float32(NEG))
        if U < 5:
            tmp = np.full((128, 12, 5, 128), NEG, np.float32)
            flat = out[vi, :, :, :U, :].reshape(128, 12 * U * 128)
            tmp.reshape(128, -1)[:, :12 * U * 128] = flat
            out[vi] = tmp
    return np.ascontiguousarray(out.reshape(len(variants), 128, -1)), var_of, kts_of


def make_inputs(S, layer_types, x, mem, w_in_a, sink_a, w_in_b, rpb_b, t5_table, w_mem_kv, w_out, w_gu, w_down,
                norm_mix_pre, norm_mix_post, norm_mem, norm_ffn_pre, norm_ffn_post):
    L = len(layer_types)
    f = lambda a: np.ascontiguousarray(np.asarray(a, dtype=np.float32))
    gc = np.zeros((128, L * 3 * 8), np.float32)
    for l in range(L):
        for kind, g in enumerate((norm_mix_pre, norm_mem, norm_ffn_pre)):
            gc[:, (l * 3 + kind) * 8:(l * 3 + kind + 1) * 8] = f(g)[l].reshape(8, 128).T
    gp = np.zeros((L * 2, D), np.float32)
    for l in range(L):
        gp[l * 2] = f(norm_mix_post)[l]
        gp[l * 2 + 1] = f(norm_ffn_post)[l]
    n_b = sum(1 for t in layer_types if t == "B")
    rp = f(rpb_b)
    tabs = []
    var_of = kts_of = None
    for jb in range(max(n_b, 1)):
        tb, var_of, kts_of = build_biasB(rp[jb], S)
        tabs.append(tb)
    biasB = np.ascontiguousarray(np.concatenate(tabs, axis=0))
    shared = dict(
        w_in_a=np.ascontiguousarray(f(w_in_a)[:, :, QPERM]), w_in_b=f(w_in_b), w_mem_kv=f(w_mem_kv), w_out=f(w_out), w_gu=f(w_gu), w_down=f(w_down),
        gcols=gc, gpost=gp, sink=f(sink_a), biasA=build_biasA(f(t5_table)), biasB=biasB,
        ident=np.eye(128, dtype=np.float32).astype(ml_dtypes.bfloat16),
    )
    return shared, biasB.shape[0] // max(n_b, 1), var_of, kts_of


_CACHE = {}
_QH = [0, 3, 1, 4, 2, 5, 6, 9, 7, 10, 8, 11]
QPERM = np.concatenate([np.arange(h * 64, (h + 1) * 64) for h in _QH] + [np.arange(768, 1536)])


def run_module(S, layer_types, inputs, n_cores=N_CORES):
    n_a = sum(1 for t in layer_types if t == "A")
    n_b = sum(1 for t in layer_types if t == "B")
    shared, nvarB, var_of, kts_of = make_inputs(S, layer_types, **inputs)
    key = (S, tuple(layer_types))
    if key not in _CACHE:
        _CACHE[key] = build_program(S, layer_types, n_a, n_b, nvarB, var_of, kts_of)
    nc, stats, hi = _CACHE[key]
    if VERBOSE:
        print("build stats", stats, "arena bytes", hi, flush=True)
    x = np.asarray(inputs["x"], dtype=np.float32)
    mem = np.asarray(inputs["mem"], dtype=np.float32)
    in_maps = []
    for c in range(n_cores):
        m = dict(shared)
        m["x"] = np.ascontiguousarray(x[c])
        m["mem"] = np.ascontiguousarray(mem[c])
        in_maps.append(m)
    res = run_bass_kernel_spmd(nc, in_maps, core_ids=list(range(n_cores)))
    return np.stack([np.asarray(r["out"]) for r in res.results], axis=0)


def kernel(x, mem, w_in_a, sink_a, w_in_b, rpb_b, t5_table, w_mem_kv, w_out, w_gu, w_down,
           norm_mix_pre, norm_mix_post, norm_mem, norm_ffn_pre, norm_ffn_post):
    inputs = dict(x=x, mem=mem, w_in_a=w_in_a, sink_a=sink_a, w_in_b=w_in_b, rpb_b=rpb_b, t5_table=t5_table,
                  w_mem_kv=w_mem_kv, w_out=w_out, w_gu=w_gu, w_down=w_down, norm_mix_pre=norm_mix_pre,
                  norm_mix_post=norm_mix_post, norm_mem=norm_mem, norm_ffn_pre=norm_ffn_pre,
                  norm_ffn_post=norm_ffn_post)
    S = np.asarray(x).shape[1]
    out = run_module(S, ["A", "B", "A", "B"], inputs)
    return out.astype(np.float32)
```

```python
import math
import numpy as np
import ml_dtypes
import concourse.bass as bass
import concourse.mybir as mybir
from concourse.bass_utils import run_bass_kernel_spmd

F32 = mybir.dt.float32
BF16 = mybir.dt.bfloat16
AF = mybir.ActivationFunctionType
ALU = mybir.AluOpType

HB = [0, 2, 4, 6, 8, 10, 1, 3, 5, 7, 9, 11]
D = 1024
HD = 64
DFF = 2816
MEM_LEN = 256
EPS = 1e-6
NEG = -30000.0
N_CORES = 8

ENGS = ("pe", "act", "dve", "pool", "sp")
SAME_ENG_SYNC = True
FFN_ON = True
VERBOSE = False
DEBUG_STAGE = 99
DEBUG_SUB = 99
DBG_NOMEM = False
DBG_NOXR = False
DBG_NOTOK = False


class Buf:
    __slots__ = ("name", "w", "r")

    def __init__(self, name=""):
        self.name = name
        self.w = None
        self.r = {}


class Op:
    __slots__ = ("eng", "fn", "deps", "sig", "semref", "val", "is_dma")


class Prog:
    def __init__(self, nc):
        self.nc = nc
        self.streams = {e: [] for e in ENGS}
        self.cur = {}
        self.nsem = 0
        self.dmas = []
        self.dsems = {}
        self.new_epoch()

    def new_sem(self, name):
        self.nsem += 1
        return [self.nc.alloc_semaphore(f"{name}_{self.nsem}"), 0]

    def dsem(self, name):
        if name not in self.dsems:
            self.dsems[name] = self.new_sem(name)
        return self.dsems[name]

    def new_epoch(self):
        for e in ENGS:
            if e == "sp":
                self.cur[e] = None
            else:
                self.cur[e] = self.new_sem("e_" + e)

    def op(self, e, fn, reads=(), writes=(), dsem=None):
        o = Op()
        o.eng = e
        o.fn = fn
        o.sig = False
        o.val = None
        o.is_dma = dsem is not None
        if o.is_dma:
            o.semref = dsem
            dsem[1] += 16
            o.val = dsem[1]
            o.sig = True
        else:
            o.semref = self.cur[e]
        deps = []

        def add(d, raw):
            if d is None or d is o:
                return
            if d.is_dma:
                deps.append(d)
                return
            if d.eng == e and not o.is_dma:
                if raw and e != "pe" and SAME_ENG_SYNC:
                    d.sig = True
                    deps.append(d)
                return
            d.sig = True
            deps.append(d)

        for b in reads:
            add(b.w, True)
        for b in writes:
            add(b.w, False)
            for r in b.r.values():
                add(r, False)
        for b in reads:
            b.r[("d", id(o)) if o.is_dma else e] = o
        for b in writes:
            b.w = o
            b.r = {}
        o.deps = deps
        self.streams[e].append(o)
        if o.is_dma:
            self.dmas.append(o)
        return o

    def barrier(self):
        lasts = [self.streams[e][-1] for e in ENGS if self.streams[e]]
        dmas = self.dmas
        self.dmas = []
        for e in ENGS:
            o = Op()
            o.eng = e
            o.fn = None
            o.sig = False
            o.val = None
            o.is_dma = False
            o.semref = self.cur[e]
            deps = []
            for d in lasts:
                if d.is_dma or d.fn is None:
                    continue
                if d.eng != e:
                    d.sig = True
                    deps.append(d)
            deps.extend(dmas)
            o.deps = deps
            self.streams[e].append(o)

    def emit(self, final_waits):
        nc = self.nc
        for e in ENGS:
            for o in self.streams[e]:
                if o.is_dma or not o.sig:
                    continue
                o.semref[1] += 1
                o.val = o.semref[1]
        stats = {}

        def run(e, eng):
            seen = {}
            nw = 0
            for o in self.streams[e]:
                for d in o.deps:
                    k = id(d.semref)
                    if seen.get(k, 0) >= d.val:
                        continue
                    eng.wait_ge(d.semref[0], d.val)
                    seen[k] = d.val
                    nw += 1
                if o.fn is None:
                    continue
                ins = o.fn(eng)
                if o.sig:
                    ins.then_inc(o.semref[0], 16 if o.is_dma else 1)
            if e == "sp":
                for d in final_waits:
                    eng.wait_ge(d.semref[0], d.val)
            stats[e] = (len(self.streams[e]), nw)

        with nc.Block() as block:
            @block.tensor
            def _(eng):
                run("pe", eng)

            @block.scalar
            def _(eng):
                run("act", eng)

            @block.vector
            def _(eng):
                run("dve", eng)

            @block.gpsimd
            def _(eng):
                run("pool", eng)

            @block.sync
            def _(eng):
                run("sp", eng)
        return stats


class Arena:
    def __init__(self, nc, nbytes):
        self.t = nc.alloc_sbuf_tensor("arena", [128, nbytes // 2], BF16)
        self.nbytes = nbytes
        self.off = 0
        self.hi = 0

    def mark(self):
        return self.off

    def reset(self, m):
        self.off = m

    def alloc(self, nelem, dtype):
        sz = 2 if dtype == BF16 else 4
        nb = (nelem * sz + 63) // 64 * 64
        assert self.off + nb <= self.nbytes, f"SBUF arena overflow {self.off}+{nb}>{self.nbytes}"
        ap = self.t[:, self.off // 2:(self.off + nelem * sz) // 2]
        self.off += nb
        self.hi = max(self.hi, self.off)
        if dtype != BF16:
            ap = ap.bitcast(dtype)
        return ap


def r3(ap, b):
    return ap.rearrange("p (a b) -> p a b", b=b)


def bcast_rows(dram_ap_row, n):
    return bass.AP(dram_ap_row.tensor, dram_ap_row.offset, [[0, 128], [1, n]])


def build_program(S, layer_types, n_a, n_b, nvarB, varB_of_tile, ktsB_of_tile):
    L = len(layer_types)
    T = S // 128
    NS = S // 512
    assert S % 512 == 0
    nc = bass.Bass("TRN2", target_bir_lowering=False)

    def din(name, shape, dt=F32):
        return nc.dram_tensor(name, list(shape), dt, kind="ExternalInput").ap()

    x_d = din("x", [S, D])
    mem_d = din("mem", [MEM_LEN, D])
    w_in_a = din("w_in_a", [max(n_a, 1), D, 1536])
    w_in_b = din("w_in_b", [max(n_b, 1), D, 2560])
    w_mem = din("w_mem_kv", [L, D, 512])
    w_out = din("w_out", [L, D, D])
    w_gu = din("w_gu", [L, D, 2 * DFF])
    w_dn = din("w_down", [L, DFF, D])
    gcols_d = din("gcols", [128, L * 3 * 8])
    gpost_d = din("gpost", [L * 2, D])
    sink_d = din("sink", [max(n_a, 1), 12])
    biasA_d = din("biasA", [128, 3 * 4 * 3 * 128])
    biasB_d = din("biasB", [max(n_b, 1) * nvarB, 128, 12 * 5 * 128])
    ident_d = din("ident", [128, 128], BF16)
    out_d = nc.dram_tensor("out", [S, D], F32, kind="ExternalOutput").ap()

    P = Prog(nc)
    AR = Arena(nc, 206 * 1024)
    PSbig = nc.alloc_psum_tensor("psbig", [128, 7 * 512], F32)[:]
    PS = [PSbig[:, i * 512:(i + 1) * 512] for i in range(7)]
    TPt = nc.alloc_psum_tensor("tp", [128, 1024], BF16)[:]
    PSB = [Buf(f"ps{i}") for i in range(7)]
    TPB = Buf("tp")

    ident = AR.alloc(128, BF16)
    gcols = AR.alloc(L * 3 * 8, F32)
    B_ident, B_gcols = Buf(), Buf()
    P.op("sp", lambda e: e.dma_start(out=ident, in_=ident_d), writes=[B_ident], dsem=P.dsem("c0"))
    P.op("sp", lambda e: e.dma_start(out=gcols, in_=gcols_d), writes=[B_gcols], dsem=P.dsem("c1"))
    epsb = AR.alloc(1, F32)
    B_epsb = Buf()
    P.op("pool", lambda e: e.memset(epsb, EPS), writes=[B_epsb])
    base_mark = AR.mark()

    XD = [Buf(f"xd{t}") for t in range(T)]

    def gcol(l, kind):
        i = (l * 3 + kind) * 8
        return gcols[:, i:i + 8]

    def rstd_from(ssq_ap, B_ssq, v_ap, B_v, rstd_ap, B_rstd, n):
        P.op("act", lambda e: e.activation(out=v_ap[:, 0:n], in_=ssq_ap, func=AF.Ln, scale=1.0 / D, bias=epsb),
             reads=[B_ssq, B_epsb], writes=[B_v])
        P.op("act", lambda e: e.activation(out=rstd_ap[:, 0:n], in_=v_ap[:, 0:n], func=AF.Exp, scale=-0.5),
             reads=[B_v], writes=[B_rstd])

    def load_w(dst3, src2d, nchunk, Bw, wsem, group=1, last=False):
        for c in range(0, nchunk, group):
            g = min(group, nchunk - c)
            src = src2d[c * 128:(c + g) * 128, :].rearrange("(c p) n -> p c n", p=128)
            dst = dst3[:, c:c + g, :]
            fin = last and (c + g >= nchunk)
            P.op("pool", lambda e, dst=dst, src=src: e.dma_start(out=dst, in_=src),
                 writes=[Bw] if fin else [], dsem=wsem)

    def attn_phase(l, typ, j, xsrc_d, first):
        P.new_epoch()
        AR.reset(base_mark)
        isA = typ == "A"
        NIN = 1536 if isA else 2560
        NQC = 6
        NKC = 2 if isA else 6
        NKV = 4 if isA else 12
        if isA:
            qcol0, kcol0, vcol0, mcol0 = 0, 768, 1024, 1280
        else:
            qcol0, kcol0, vcol0, mcol0 = 0, 768, 1536, 2304
        wsem = P.dsem("w")
        WIN = r3(AR.alloc(8 * NIN, BF16), NIN)
        WOUT = r3(AR.alloc(8 * D, BF16), D)
        B_W = Buf()
        B_WIN = B_WOUT = B_WMEM = B_W
        w_in = (w_in_a if isA else w_in_b)[j]
        WMEM = r3(AR.alloc(8 * 512, BF16), 512)
        load_w(WMEM, w_mem[l], 8, B_WMEM, wsem, group=4)
        load_w(WIN, w_in, 8, B_WIN, wsem, group=1)
        load_w(WOUT, w_out[l], 8, B_WOUT, wsem, group=2, last=True)
        gpost = AR.alloc(D, F32)
        B_gpost = Buf()
        P.op("sp", lambda e: e.dma_start(out=gpost, in_=bcast_rows(gpost_d[l * 2:l * 2 + 1, :], D)),
             writes=[B_gpost], dsem=P.dsem("g"))
        esink = AR.alloc(16, F32)
        B_esink = Buf()
        P.op("pool", lambda e: e.memset(esink, 0.0), writes=[B_esink])
        if isA:
            sraw = AR.alloc(16, F32)
            B_sraw = Buf()
            P.op("sp", lambda e: e.dma_start(out=sraw[:, 0:12], in_=bcast_rows(sink_d[j:j + 1, :], 12)),
                 writes=[B_sraw], dsem=P.dsem("sr"))
            P.op("act", lambda e: e.activation(out=esink[:, 0:12], in_=sraw[:, 0:12], func=AF.Exp),
                 reads=[B_sraw], writes=[B_esink])
        if isA:
            NBIAS = 3 * 4 * 3 * 128
            bias = AR.alloc(NBIAS, F32)
            B_bias = Buf()
            P.op("sp", lambda e: e.dma_start(out=bias, in_=biasA_d), writes=[B_bias], dsem=P.dsem("bias"))
        else:
            NBIAS = 12 * 5 * 128
            bias = AR.alloc(NBIAS, F32)
            B_bias = Buf()
            bias_state = {"var": None}

            def ensure_bias(var):
                if bias_state["var"] == var:
                    return
                bias_state["var"] = var
                src = biasB_d[j * nvarB + var]
                P.op("sp", lambda e, src=src: e.dma_start(out=bias, in_=src), writes=[B_bias], dsem=P.dsem("bias"))
        NXP = 2
        XP = [AR.alloc(D, F32) for _ in range(NXP)]
        B_XP = [Buf() for _ in range(NXP)]
        xpsem = [P.dsem(f"xp{i}") for i in range(NXP)]
        XN = [AR.alloc(D, BF16) for _ in range(2)]
        B_XN = [Buf() for _ in range(2)]
        junk = AR.alloc(D, BF16)
        B_junk = Buf()
        small = AR.alloc(64, F32)
        ssq_p = [small[:, 0:1], small[:, 1:2]]
        v_p = [small[:, 2:3], small[:, 3:4]]
        rs_p = [small[:, 4:5], small[:, 5:6]]
        B_ssq_p = [Buf(), Buf()]
        B_v_p = [Buf(), Buf()]
        B_rs_p = [Buf(), Buf()]
        ssq_o = [small[:, 8:10], small[:, 10:12]]
        v_o = [small[:, 12:15], small[:, 16:19]]
        rs_o = [small[:, 20:21], small[:, 21:22]]
        B_ssq_o = [Buf(), Buf()]
        B_v_o = [Buf(), Buf()]
        B_rs_o = [Buf(), Buf()]
        dsum = [small[:, 24:40], small[:, 40:56]]
        B_dsum = [Buf(), Buf()]
        rcp = [AR.alloc(16, F32), AR.alloc(16, F32)]
        B_rcp = [Buf(), Buf()]

        HT = r3(AR.alloc(8 * 512, BF16), 512)
        B_HT = [Buf() for _ in range(4)]
        NQS = 2
        QT = [r3(AR.alloc(NQC * 512, BF16), 512) for _ in range(NQS)]
        QM = [r3(AR.alloc(2 * 512, BF16), 512) for _ in range(NQS)]
        B_QT = [Buf() for _ in range(NQS)]
        B_QM = [Buf() for _ in range(NQS)]
        NKS = 3
        KT = [r3(AR.alloc(NKC * 512, BF16), 512) for _ in range(NKS)]
        B_KT = [Buf() for _ in range(NKS)]
        VT = [[r3(AR.alloc(NKV * 65, BF16), 65) for _ in range(4)] for _ in range(NKS)]
        B_VT = [[Buf() for _ in range(4)] for _ in range(NKS)]
        for ks in range(NKS):
            for i in range(4):
                P.op("pool", lambda e, ap=VT[ks][i]: e.memset(ap, 1.0), writes=[B_VT[ks][i]])
        NPT = 5
        PT = [AR.alloc(512, BF16) for _ in range(NPT)]
        B_PT = [Buf() for _ in range(NPT)]
        CAT = [AR.alloc(D, BF16) for _ in range(2)]
        B_CAT = [Buf(), Buf()]
        CATT = [r3(AR.alloc(D, BF16), 128) for _ in range(2)]
        B_CATT = [Buf(), Buf()]
        YSB = AR.alloc(D, F32)
        B_YSB = Buf()
        MKT = r3(AR.alloc(2 * 256, BF16), 256)
        B_MKT = Buf()
        MV = [r3(AR.alloc(4 * 65, BF16), 65) for _ in range(2)]
        B_MV = Buf()
        for mt in range(2):
            P.op("pool", lambda e, ap=MV[mt]: e.memset(ap, 1.0), writes=[B_MV])
        XR = [AR.alloc(D, F32) for _ in range(2)]
        B_XR = [Buf(), Buf()]

        S_ring = [(PS[0], PSB[0]), (PS[1], PSB[1])]
        O_banks = [(PS[2], PSB[2]), (PS[3], PSB[3]), (PS[4], PSB[4])]
        YP = [(PS[5], PSB[5]), (PS[6], PSB[6])]
        YPfull = PSbig[:, 5 * 512:7 * 512]
        TP3 = r3(TPt, 128)

        def o_ap(h, lo, hi):
            b, i = divmod(h, 7)
            return O_banks[b][0][:, i * 65 + lo:i * 65 + hi], O_banks[b][1]

        cnt = {"pt": 0, "yp": 0, "q": 0}

        def norm_a(src_rows_ap, src_bufs):
            k = cnt["pt"]
            cnt["pt"] += 1
            s2 = k % 2
            sl = k % NXP
            P.op("sp", lambda e: e.dma_start(out=XP[sl], in_=src_rows_ap), reads=src_bufs,
                 writes=[B_XP[sl]], dsem=xpsem[sl])
            P.op("act", lambda e: e.activation(out=junk, in_=XP[sl], func=AF.Square, accum_out=ssq_p[s2]),
                 reads=[B_XP[sl]], writes=[B_junk, B_ssq_p[s2]])
            rstd_from(ssq_p[s2], B_ssq_p[s2], v_p[s2], B_v_p[s2], rs_p[s2], B_rs_p[s2], 1)
            P.op("dve", lambda e: e.tensor_scalar(out=XN[s2], in0=XP[sl], scalar1=rs_p[s2], scalar2=None,
                                                  op0=ALU.mult),
                 reads=[B_XP[sl], B_rs_p[s2]], writes=[B_XN[s2]])
            return s2

        def norm_b(s2, dst3, col0, B_dst, gkind):
            for c in range(8):
                P.op("pe", lambda e, c=c: e.transpose(out=TP3[:, c, :], in_=XN[s2][:, c * 128:(c + 1) * 128],
                                                      identity=ident),
                     reads=[B_XN[s2], B_ident], writes=[TPB])
            gb = gcol(l, gkind).unsqueeze(2).to_broadcast([128, 8, 128])
            P.op("dve", lambda e: e.tensor_tensor(out=dst3[:, :, col0:col0 + 128], in0=TP3, in1=gb, op=ALU.mult),
                 reads=[TPB, B_gcols], writes=[B_dst])

        def norm_transpose(src_rows_ap, src_bufs, dst3, col0, B_dst, gkind):
            s2 = norm_a(src_rows_ap, src_bufs)
            norm_b(s2, dst3, col0, B_dst, gkind)

        def next_yp():
            k = cnt["yp"]
            cnt["yp"] += 1
            return YP[k % 2]

        MT = HT
        for mt in range(2):
            norm_transpose(mem_d[mt * 128:(mt + 1) * 128, :], [], MT, mt * 128, B_HT[mt], 1)
        for mc in range(2):
            yp, B_yp = next_yp()
            for c in range(8):
                P.op("pe", lambda e, c=c, mc=mc, yp=yp: e.matmul(yp[:, 0:256], lhsT=WMEM[:, c, mc * 128:(mc + 1) * 128],
                                                                 rhs=MT[:, c, 0:256], start=(c == 0), stop=(c == 7)),
                     reads=[B_WMEM, B_HT[0], B_HT[1]], writes=[B_yp])
            P.op("act", lambda e, mc=mc, yp=yp: e.activation(out=MKT[:, mc, :], in_=yp[:, 0:256], func=AF.Copy),
                 reads=[B_yp], writes=[B_MKT])
        for mt in range(2):
            yp, B_yp = next_yp()
            for c in range(8):
                P.op("pe", lambda e, c=c, mt=mt, yp=yp: e.matmul(yp[:, 0:256], lhsT=MT[:, c, mt * 128:(mt + 1) * 128],
                                                                 rhs=WMEM[:, c, 256:512], start=(c == 0), stop=(c == 7)),
                     reads=[B_WMEM, B_HT[mt]], writes=[B_yp])
            P.op("dve", lambda e, mt=mt, yp=yp: e.tensor_copy(out=MV[mt][:, :, 0:64], in_=r3(yp[:, 0:256], 64)),
                 reads=[B_yp], writes=[B_MV])
        if DEBUG_STAGE <= 2:
            P.barrier()
            return
        NXR = 2
        xrsem = [P.dsem(f"xr{i}") for i in range(NXR)]
        xssem = [P.dsem(f"xs{i}") for i in range(NXR)]

        pt_slot = {}

        def ptile_a(t):
            pt_slot[t] = norm_a(xsrc_d[t * 128:(t + 1) * 128, :], [XD[t]] if not first else [])

        def ptile(t):
            ptile_a(t)
            ptile_b(t)

        def ptile_b(t):
            i = t % 4
            ks = (t // 4) % NKS
            norm_b(pt_slot[t], HT, i * 128, B_HT[i], 0)
            if isA:
                groups = [(0, 4)]
            else:
                groups = [(0, 6), (6, 12)]
            for (h0, h1) in groups:
                yp, B_yp = next_yp()
                n = (h1 - h0) * 64
                for c in range(8):
                    P.op("pe", lambda e, c=c, yp=yp, n=n, h0=h0: e.matmul(
                        yp[:, 0:n], lhsT=HT[:, c, i * 128:(i + 1) * 128],
                        rhs=WIN[:, c, vcol0 + h0 * 64:vcol0 + h0 * 64 + n], start=(c == 0), stop=(c == 7)),
                        reads=[B_WIN, B_HT[i]], writes=[B_yp])
                P.op("dve", lambda e, yp=yp, n=n, h0=h0, h1=h1: e.tensor_copy(
                    out=VT[ks][i][:, h0:h1, 0:64], in_=r3(yp[:, 0:n], 64)),
                    reads=[B_yp], writes=[B_VT[ks][i]])

        def chunk(dst, B_dst, lhsT_of_c, scale, eng):
            yp, B_yp = next_yp()
            for c in range(8):
                P.op("pe", lambda e, c=c, yp=yp: e.matmul(yp, lhsT=lhsT_of_c(c), rhs=HT[:, c, :],
                                                         start=(c == 0), stop=(c == 7)),
                     reads=[B_WIN] + B_HT, writes=[B_yp])
            if eng == "act":
                P.op("act", lambda e, yp=yp: e.activation(out=dst, in_=yp, func=AF.Copy, scale=scale),
                     reads=[B_yp], writes=[B_dst])
            else:
                P.op("dve", lambda e, yp=yp: e.tensor_scalar(out=dst, in0=yp, scalar1=scale, scalar2=None,
                                                             op0=ALU.mult),
                     reads=[B_yp], writes=[B_dst])

        def kchunks(s):
            ks = s % NKS
            for kc in range(NKC):
                chunk(KT[ks][:, kc, :], B_KT[ks],
                      lambda c, kc=kc: WIN[:, c, kcol0 + kc * 128:kcol0 + (kc + 1) * 128], 1.0, "dve")

        QH = [0, 1, 2, 6, 7, 8]

        def qchunks(s, part=None):
            qs = s % NQS
            for qc in range(NQC):
                if part is not None and (qc < 4) != (part == 0):
                    continue
                def lf(c, qc=qc):
                    return WIN[:, c, qcol0 + qc * 128:qcol0 + (qc + 1) * 128]
                chunk(QT[qs][:, qc, :], B_QT[qs], lf, 0.125, "act")
            for mc in range(2):
                if part == 0:
                    continue
                chunk(QM[qs][:, mc, :], B_QM[qs],
                      lambda c, mc=mc: WIN[:, c, mcol0 + mc * 128:mcol0 + (mc + 1) * 128], 0.125, "act")

        def attn(qt):
            s = qt // 4
            qi = qt % 4
            qs = s % NQS
            k2 = cnt["q"] % 2
            cnt["q"] += 1
            qcols = slice(qi * 128, (qi + 1) * 128)
            xr = qt % NXR
            if not DBG_NOXR:
              P.op("sp", lambda e: e.dma_start(out=XR[xr], in_=xsrc_d[qt * 128:(qt + 1) * 128, :]),
                 reads=[XD[qt]] if not first else [], writes=[B_XR[xr]], dsem=xrsem[xr])
            units = []
            if DBG_NOTOK:
                pass
            elif isA:
                kts = [kt for kt in (qt - 1, qt, qt + 1) if 0 <= kt < T]
                for kv in range(4):
                    off = 64 * (kv % 2)
                    c0 = 0 if kv < 2 else 3
                    for kt in kts:
                        jj = kt - qt + 1
                        ks = (kt // 4) % NKS
                        ki = kt % 4
                        units.append(dict(
                            N=384,
                            lhsT=KT[ks][off:off + 64, kv // 2, ki * 128:(ki + 1) * 128],
                            rhs=QT[qs][off:off + 64, c0:c0 + 3, qcols],
                            bias=True, bcol=((jj * 4 + kv) * 3) * 128, off=off,
                            subs=[(kv * 3 + g, g * 128) for g in range(3)],
                            v=VT[ks][ki][:, kv, :], reads=[B_KT[ks], B_QT[qs]], vreads=[B_VT[ks][ki]]))
            else:
                kts = ktsB_of_tile[qt]
                U = len(kts)
                ensure_bias(varB_of_tile[qt])
                for pos, h in enumerate(HB):
                    off = 64 * (h % 2)
                    for u, kt in enumerate(kts):
                        ks = (kt // 4) % NKS
                        ki = kt % 4
                        units.append(dict(
                            N=128,
                            lhsT=KT[ks][off:off + 64, h // 2, ki * 128:(ki + 1) * 128],
                            rhs=QT[qs][off:off + 64, h // 2, qcols],
                            bias=True, bcol=(pos * U + u) * 128, off=off,
                            subs=[(h, 0)],
                            v=VT[ks][ki][:, h, :], reads=[B_KT[ks], B_QT[qs]], vreads=[B_VT[ks][ki]]))
            for mh in ([0, 2, 1, 3] if not DBG_NOMEM else []):
                off = 64 * (mh % 2)
                for mt in range(2):
                    units.append(dict(
                        N=128,
                        lhsT=MKT[off:off + 64, mh // 2, mt * 128:(mt + 1) * 128],
                        rhs=QM[qs][off:off + 64, mh // 2, qcols],
                        bias=None, off=off, subs=[(12 + mh, 0)],
                        v=MV[mt][:, mh, :], reads=[B_MKT, B_QM[qs]], vreads=[B_MV]))
            banks = []
            cur = []
            curn = 0
            for u in units:
                hasb = u["bias"] is not None
                if cur and (curn + u["N"] > 512 or (cur[0]["bias"] is not None) != hasb or cur[0]["off"] != u["off"]):
                    banks.append(cur)
                    cur = []
                    curn = 0
                u["col"] = curn
                cur.append(u)
                curn += u["N"]
            if cur:
                banks.append(cur)
            head_units = {}
            for bi, bk in enumerate(banks):
                for u in bk:
                    u["bank"] = bi
                    for (h, co) in u["subs"]:
                        head_units.setdefault(h, []).append((u, co))
            head_last_bank = {h: max(u["bank"] for (u, _) in lst) for h, lst in head_units.items()}
            gb = cnt.setdefault("bank", 0)

            def emit_pv(bi):
                for h in sorted(head_units):
                    if head_last_bank[h] != bi:
                        continue
                    lst = head_units[h]
                    oap, B_o = o_ap(h, 0, 65)
                    for n, (u, co) in enumerate(lst):
                        pslot = (gb + u["bank"]) % NPT
                        c0 = u["col"] + co
                        P.op("pe", lambda e, u=u, pslot=pslot, c0=c0, n=n, oap=oap, last=(n == len(lst) - 1):
                             e.matmul(oap, lhsT=PT[pslot][:, c0:c0 + 128], rhs=u["v"], start=(n == 0), stop=last),
                             reads=[B_PT[pslot]] + u["vreads"], writes=[B_o])

            for bi, bk in enumerate(banks):
                sb, B_sb = S_ring[(gb + bi) % 2]
                ncols = sum(u["N"] for u in bk)
                for u in bk:
                    P.op("pe", lambda e, u=u, sb=sb: e.matmul(sb[:, u["col"]:u["col"] + u["N"]], lhsT=u["lhsT"],
                                                             rhs=u["rhs"], start=True, stop=True),
                         reads=u["reads"], writes=[B_sb])
                if bk[0]["bias"] is not None:
                    bc0 = bk[0]["bcol"]
                    assert all(u["bcol"] == bc0 + u["col"] for u in bk)
                    bias_ap = bias[:, bc0:bc0 + ncols]
                    P.op("dve", lambda e, sb=sb, ncols=ncols, bias_ap=bias_ap: e.tensor_tensor(
                        out=sb[:, 0:ncols], in0=sb[:, 0:ncols], in1=bias_ap, op=ALU.add),
                        reads=[B_sb, B_bias], writes=[B_sb])
                pslot = (gb + bi) % NPT
                P.op("act", lambda e, sb=sb, ncols=ncols, pslot=pslot: e.activation(
                    out=PT[pslot][:, 0:ncols], in_=sb[:, 0:ncols], func=AF.Exp),
                    reads=[B_sb], writes=[B_PT[pslot]])
                if bi >= 1 and DEBUG_SUB >= 2:
                    emit_pv(bi - 1)
                if bi == 2 and pending_post:
                    pending_post.pop(0)()
            if DEBUG_SUB >= 2:
                emit_pv(len(banks) - 1)
            cnt["bank"] = gb + len(banks)
            ds, B_ds = dsum[k2], B_dsum[k2]
            rc, B_rc = rcp[k2], B_rcp[k2]
            for b, (h0, h1) in enumerate([(0, 7), (7, 14), (14, 16)]):
                n = h1 - h0
                ob3 = r3(O_banks[b][0][:, 0:n * 65], 65)
                P.op("dve", lambda e, ob3=ob3, n=n, h0=h0, h1=h1: e.tensor_tensor(
                    out=ds[:, h0:h1], in0=ob3[:, :, 64], in1=esink[:, h0:h1], op=ALU.add),
                    reads=[O_banks[b][1], B_esink], writes=[B_ds])
            P.op("dve", lambda e: e.reciprocal(out=rc, in_=ds), reads=[B_ds], writes=[B_rc])
            cat, B_cat = CAT[k2], B_CAT[k2]
            for b, (h0, h1) in enumerate([(0, 7), (7, 14), (14, 16)]):
                n = h1 - h0
                ob3 = r3(O_banks[b][0][:, 0:n * 65], 65)
                P.op("dve", lambda e, ob3=ob3, n=n, h0=h0, h1=h1: e.tensor_tensor(
                    out=r3(cat[:, h0 * 64:h1 * 64], 64), in0=ob3[:, :, 0:64],
                    in1=rc[:, h0:h1].unsqueeze(2).to_broadcast([128, n, 64]), op=ALU.mult),
                    reads=[O_banks[b][1], B_rc], writes=[B_cat])
            yield
            for c in range(8):
                P.op("pe", lambda e, c=c: e.transpose(out=TP3[:, c, :], in_=cat[:, c * 128:(c + 1) * 128],
                                                      identity=ident),
                     reads=[B_cat, B_ident], writes=[TPB])
            catt, B_catt = CATT[k2], B_CATT[k2]
            P.op("act", lambda e: e.activation(out=catt, in_=TP3, func=AF.Copy), reads=[TPB], writes=[B_catt])
            for half in range(2):
                yp, B_yp = YP[half]
                for c in range(8):
                    P.op("pe", lambda e, c=c, yp=yp, half=half: e.matmul(
                        yp, lhsT=catt[:, c, :], rhs=WOUT[:, c, half * 512:(half + 1) * 512],
                        start=(c == 0), stop=(c == 7)),
                        reads=[B_catt, B_WOUT], writes=[B_yp])
            cnt["yp"] = 0
            yield
            P.op("act", lambda e: e.activation(out=junk, in_=YPfull, func=AF.Square, accum_out=ssq_o[k2][:, 0:1]),
                 reads=[YP[0][1], YP[1][1]], writes=[B_junk, B_ssq_o[k2]])
            rstd_from(ssq_o[k2][:, 0:1], B_ssq_o[k2], v_o[k2], B_v_o[k2], rs_o[k2], B_rs_o[k2], 1)
            for half in range(2):
                yp, B_yp = YP[half]
                P.op("dve", lambda e, yp=yp, half=half: e.scalar_tensor_tensor(
                    out=YSB[:, half * 512:(half + 1) * 512], in0=yp, scalar=rs_o[k2],
                    in1=gpost[:, half * 512:(half + 1) * 512], op0=ALU.mult, op1=ALU.mult),
                    reads=[B_yp, B_rs_o[k2], B_gpost], writes=[B_YSB])
            P.op("pool", lambda e: e.tensor_tensor(out=XR[xr], in0=XR[xr], in1=YSB, op=ALU.add),
                 reads=[B_XR[xr], B_YSB], writes=[B_XR[xr]])
            P.op("sp", lambda e: e.dma_start(out=out_d[qt * 128:(qt + 1) * 128, :], in_=XR[xr]),
                 reads=[B_XR[xr]], writes=[XD[qt]], dsem=xssem[xr])

        for t in range(4):
            ptile(t)
        if DEBUG_STAGE <= 3:
            P.barrier()
            return
        kchunks(0)
        qchunks(0)
        if DEBUG_STAGE <= 4:
            P.barrier()
            return
        pending_post = []

        def run_attn(qt, pre_fns, between):
            for f in pre_fns:
                f()
            g = attn(qt)
            next(g)
            while pending_post:
                pending_post.pop(0)()
            for f in between:
                f()
            next(g)

            def post(g=g):
                for _ in g:
                    pass
            pending_post.append(post)

        for s in range(NS):
            nxt = s + 1 < NS
            t1 = 4 * (s + 1)
            run_attn(4 * s + 0, [lambda: ptile_a(t1 + 0), lambda: ptile_a(t1 + 1)] if nxt else [],
                     [lambda: ptile_b(t1 + 0), lambda: ptile_b(t1 + 1)] if nxt else [])
            run_attn(4 * s + 1, [lambda: ptile_a(t1 + 2), lambda: ptile_a(t1 + 3)] if nxt else [],
                     [lambda: ptile_b(t1 + 2), lambda: ptile_b(t1 + 3), lambda: kchunks(s + 1)] if nxt else [])
            run_attn(4 * s + 2, [], [lambda: qchunks(s + 1, 0)] if nxt else [])
            run_attn(4 * s + 3, [], [lambda: qchunks(s + 1, 1)] if nxt else [])
        while pending_post:
            pending_post.pop(0)()
        P.barrier()

    def ffn_phase(l):
        P.new_epoch()
        AR.reset(base_mark)
        FT = 256
        NIT = S // FT
        wsem = P.dsem("w")
        WGU = r3(AR.alloc(8 * 2 * DFF, BF16), 2 * DFF)
        WDN = r3(AR.alloc(22 * D, BF16), D)
        B_WGU = B_WDN = Buf()
        load_w(WGU, w_gu[l], 8, B_WGU, wsem, group=1)
        load_w(WDN, w_dn[l], 22, B_WDN, wsem, group=4, last=True)
        gpost = AR.alloc(D, F32)
        B_gpost = Buf()
        P.op("sp", lambda e: e.dma_start(out=gpost, in_=bcast_rows(gpost_d[l * 2 + 1:l * 2 + 2, :], D)),
             writes=[B_gpost], dsem=P.dsem("g"))
        NXS = 3
        XS = [r3(AR.alloc(2 * D, F32), D) for _ in range(NXS)]
        B_XS = [Buf() for _ in range(NXS)]
        xlsem = [P.dsem(f"xl{i}") for i in range(NXS)]
        xssem = [P.dsem(f"xs{i}") for i in range(NXS)]
        XN = [AR.alloc(D, BF16) for _ in range(2)]
        B_XN = [Buf(), Buf()]
        junk = AR.alloc(D, BF16)
        B_junk = Buf()
        HT = [r3(AR.alloc(8 * FT, BF16), FT) for _ in range(2)]
        B_HT = [[Buf(), Buf()] for _ in range(2)]
        ACTT = r3(AR.alloc(22 * FT, BF16), FT)
        B_ACTT = [Buf(), Buf()]
        TH = [AR.alloc(FT, F32) for _ in range(2)]
        B_TH = [Buf(), Buf()]
        A1 = [AR.alloc(FT, F32) for _ in range(2)]
        B_A1 = [Buf(), Buf()]
        YSB = AR.alloc(D, F32)
        B_YSB = Buf()
        small = AR.alloc(64, F32)
        ssq_p = [small[:, 0:2], small[:, 2:4]]
        v_p = [small[:, 4:6], small[:, 6:8]]
        rs_p = [small[:, 8:10], small[:, 10:12]]
        B_ssq_p, B_v_p, B_rs_p = [Buf(), Buf()], [Buf(), Buf()], [Buf(), Buf()]
        ssq_o = [small[:, 16:18], small[:, 18:20]]
        v_o = [small[:, 20:23], small[:, 24:27]]
        rs_o = [small[:, 28:29], small[:, 29:30]]
        B_ssq_o, B_v_o, B_rs_o = [Buf(), Buf()], [Buf(), Buf()], [Buf(), Buf()]
        TP3 = r3(TPt, 128)
        GU = [(PS[0], PSB[0]), (PS[1], PSB[1]), (PS[2], PSB[2])]
        Y = [[(PS[3], PSB[3]), (PS[4], PSB[4])], [(PS[5], PSB[5]), (PS[6], PSB[6])]]
        gb = gcol(l, 2).unsqueeze(2).to_broadcast([128, 8, 128])
        nfcc = {"n": 0}

        def pre(it):
            sl = it % NXS
            k2 = it % 2
            t0 = it * 2
            src = out_d[it * FT:(it + 1) * FT, :].rearrange("(t p) d -> p t d", p=128)
            P.op("sp", lambda e, sl=sl, src=src: e.dma_start(out=XS[sl], in_=src),
                 reads=[XD[t0], XD[t0 + 1]], writes=[B_XS[sl]], dsem=xlsem[sl])
            for jx in range(2):
                P.op("act", lambda e, sl=sl, jx=jx, k2=k2: e.activation(
                    out=junk, in_=XS[sl][:, jx, :], func=AF.Square, accum_out=ssq_p[k2][:, jx:jx + 1]),
                    reads=[B_XS[sl]], writes=[B_junk, B_ssq_p[k2]])
            rstd_from(ssq_p[k2], B_ssq_p[k2], v_p[k2], B_v_p[k2], rs_p[k2], B_rs_p[k2], 2)
            ht, B_ht = HT[k2], B_HT[k2]
            for jx in range(2):
                P.op("act", lambda e, sl=sl, jx=jx, k2=k2: e.activation(
                    out=XN[jx], in_=XS[sl][:, jx, :], func=AF.Copy, scale=rs_p[k2][:, jx:jx + 1]),
                    reads=[B_XS[sl], B_rs_p[k2]], writes=[B_XN[jx]])
                for c in range(8):
                    P.op("pe", lambda e, c=c, jx=jx: e.transpose(out=TP3[:, c, :],
                                                                 in_=XN[jx][:, c * 128:(c + 1) * 128],
                                                                 identity=ident),
                         reads=[B_XN[jx], B_ident], writes=[TPB])
                P.op("dve", lambda e, jx=jx, ht=ht: e.tensor_tensor(out=ht[:, :, jx * 128:(jx + 1) * 128], in0=TP3,
                                                                    in1=gb, op=ALU.mult),
                     reads=[TPB, B_gcols], writes=[B_ht[jx]])

        pre(0)
        if NIT > 1:
            pre(1)
        for it in range(NIT):
            sl = it % NXS
            k2 = it % 2
            t0 = it * 2
            ht, B_ht = HT[k2], B_HT[k2]
            nfc = nfcc["n"]
            for fc in range(22):
                gu, B_gu = GU[nfc % 3]
                t2 = nfc % 2
                nfc += 1
                for part in range(2):
                    col = part * DFF + fc * 128
                    for c in range(8):
                        P.op("pe", lambda e, c=c, gu=gu, part=part, col=col, ht=ht: e.matmul(
                            gu[:, part * FT:(part + 1) * FT], lhsT=WGU[:, c, col:col + 128], rhs=ht[:, c, :],
                            start=(c == 0), stop=(c == 7)),
                            reads=[B_WGU, B_ht[0], B_ht[1]], writes=[B_gu])
                P.op("act", lambda e, gu=gu, t2=t2: e.activation(out=TH[t2], in_=gu[:, 0:FT], func=AF.Tanh,
                                                                 scale=0.5),
                     reads=[B_gu], writes=[B_TH[t2]])
                P.op("dve", lambda e, gu=gu, t2=t2: e.scalar_tensor_tensor(
                    out=A1[t2], in0=TH[t2], scalar=1.0, in1=gu[:, 0:FT], op0=ALU.add, op1=ALU.mult),
                    reads=[B_TH[t2], B_gu], writes=[B_A1[t2]])
                P.op("dve", lambda e, gu=gu, t2=t2, fc=fc: e.scalar_tensor_tensor(
                    out=ACTT[:, fc, :], in0=A1[t2], scalar=0.5, in1=gu[:, FT:2 * FT], op0=ALU.mult, op1=ALU.mult),
                    reads=[B_A1[t2], B_gu], writes=[B_ACTT[0], B_ACTT[1]])
            nfcc["n"] = nfc
            if it + 2 < NIT:
                pre(it + 2)
            for jx in range(2):
                yb = Y[jx]
                for half in range(2):
                    yp, B_yp = yb[half]
                    for fc in range(22):
                        P.op("pe", lambda e, fc=fc, yp=yp, half=half, jx=jx: e.matmul(
                            yp, lhsT=ACTT[:, fc, jx * 128:(jx + 1) * 128],
                            rhs=WDN[:, fc, half * 512:(half + 1) * 512], start=(fc == 0), stop=(fc == 21)),
                            reads=[B_ACTT[jx], B_WDN], writes=[B_yp])
                yfull = PSbig[:, (3 + 2 * jx) * 512:(5 + 2 * jx) * 512]
                P.op("act", lambda e, yfull=yfull, jx=jx: e.activation(out=junk, in_=yfull, func=AF.Square,
                                                                       accum_out=ssq_o[jx][:, 0:1]),
                     reads=[yb[0][1], yb[1][1]], writes=[B_junk, B_ssq_o[jx]])
                rstd_from(ssq_o[jx][:, 0:1], B_ssq_o[jx], v_o[jx], B_v_o[jx], rs_o[jx], B_rs_o[jx], 1)
                for half in range(2):
                    yp, B_yp = yb[half]
                    P.op("dve", lambda e, yp=yp, half=half, jx=jx: e.scalar_tensor_tensor(
                        out=YSB[:, half * 512:(half + 1) * 512], in0=yp, scalar=rs_o[jx],
                        in1=gpost[:, half * 512:(half + 1) * 512], op0=ALU.mult, op1=ALU.mult),
                        reads=[B_yp, B_rs_o[jx], B_gpost], writes=[B_YSB])
                P.op("pool", lambda e, sl=sl, jx=jx: e.tensor_tensor(out=XS[sl][:, jx, :], in0=XS[sl][:, jx, :],
                                                                      in1=YSB, op=ALU.add),
                     reads=[B_XS[sl], B_YSB], writes=[B_XS[sl]])
            dst = out_d[it * FT:(it + 1) * FT, :].rearrange("(t p) d -> p t d", p=128)
            P.op("sp", lambda e, sl=sl, dst=dst: e.dma_start(out=dst, in_=XS[sl]),
                 reads=[B_XS[sl]], writes=[XD[t0], XD[t0 + 1]], dsem=xssem[sl])
        P.barrier()

    ia = ib = 0
    for l, typ in enumerate(layer_types):
        if typ == "A":
            attn_phase(l, "A", ia, x_d if l == 0 else out_d, l == 0)
            ia += 1
        elif typ == "B":
            attn_phase(l, "B", ib, x_d if l == 0 else out_d, l == 0)
            ib += 1
        if FFN_ON:
            ffn_phase(l)
    final = [XD[t].w for t in range(T) if XD[t].w is not None]
    stats = P.emit(final)
    return nc, stats, AR.hi


def t5_bucket_np(rel):
    half = 16
    max_exact = 8
    n = -rel
    ret = np.where(n < 0, half, 0)
    n = np.abs(n)
    nf = np.maximum(n, 1).astype(np.float32)
    large = max_exact + (np.log(nf / np.float32(max_exact)) / np.float32(math.log(128 / max_exact))
                         * np.float32(half - max_exact)).astype(np.int32)
    large = np.minimum(large, half - 1)
    return ret + np.where(n < max_exact, n, large)


def build_biasA(t5_table):
    k = np.arange(128)[:, None]
    q = np.arange(128)[None, :]
    out = np.full((128, 3, 4, 3, 128), NEG, np.float32)
    for jj in range(3):
        rel = (jj - 1) * 128 + k - q
        ok = np.abs(rel) <= 128
        bk = t5_bucket_np(rel.astype(np.int32))
        for kv in range(4):
            for g in range(3):
                h = kv * 3 + g
                vals = t5_table[bk, h]
                out[:, jj, kv, g, :] = np.where(ok, vals, np.float32(NEG))
    return np.ascontiguousarray(out.reshape(128, -1))


def natten_plan(S):
    rows = S // 64
    T = S // 128
    kr = min(8, rows)

    def rs(r):
        return int(np.clip(r - kr // 2, 0, rows - kr))
    variants = []
    var_of = []
    kts_of = []
    for t in range(T):
        r0, r1 = 2 * t, 2 * t + 1
        lo = rs(r0)
        hi = rs(r1) + kr - 1
        kts = list(range(lo // 2, hi // 2 + 1))
        key = (rs(r0) - r0, rs(r1) - r1, kts[0] - t, len(kts))
        if key not in variants:
            variants.append(key)
        var_of.append(variants.index(key))
        kts_of.append(kts)
    return variants, var_of, kts_of, kr


def build_biasB(rpb, S):
    variants, var_of, kts_of, kr = natten_plan(S)
    out = np.full((len(variants), 128, 12, 5, 128), NEG, np.float32)
    kk = np.arange(128)
    rk = kk // 64
    ck = kk % 64
    rq = kk // 64
    cq = kk % 64
    cs = np.clip(cq - 8, 0, 64 - 16)
    for vi, (o0, o1, kt0, U) in enumerate(variants):
        ro = np.where(rq == 0, o0, o1)
        for u in range(U):
            drow = (2 * (kt0 + u) + rk)[:, None] - rq[None, :]
            rel_start = ro[None, :]
            okr = (drow >= rel_start) & (drow <= rel_start + kr - 1)
            okc = (ck[:, None] >= cs[None, :]) & (ck[:, None] <= cs[None, :] + 15)
            ok = okr & okc
            ri = np.clip(drow + 7, 0, 14)
            ci = np.clip(ck[:, None] - cq[None, :] + 15, 0, 30)
            for pos, h in enumerate(HB):
                vals = rpb[h][ri, ci]
                out[vi, :, pos, u, :] = np.where(ok, vals, np.float32(NEG))
        if U < 5:
            tmp = np.full((128, 12, 5, 128), NEG, np.float32)
            flat = out[vi, :, :, :U, :].reshape(128, 12 * U * 128)
            tmp.reshape(128, -1)[:, :12 * U * 128] = flat
            out[vi] = tmp
    return np.ascontiguousarray(out.reshape(len(variants), 128, -1)), var_of, kts_of


def make_inputs(S, layer_types, x, mem, w_in_a, sink_a, w_in_b, rpb_b, t5_table, w_mem_kv, w_out, w_gu, w_down,
                norm_mix_pre, norm_mix_post, norm_mem, norm_ffn_pre, norm_ffn_post):
    L = len(layer_types)
    f = lambda a: np.ascontiguousarray(np.asarray(a, dtype=np.float32))
    gc = np.zeros((128, L * 3 * 8), np.float32)
    for l in range(L):
        for kind, g in enumerate((norm_mix_pre, norm_mem, norm_ffn_pre)):
            gc[:, (l * 3 + kind) * 8:(l * 3 + kind + 1) * 8] = f(g)[l].reshape(8, 128).T
    gp = np.zeros((L * 2, D), np.float32)
    for l in range(L):
        gp[l * 2] = f(norm_mix_post)[l]
        gp[l * 2 + 1] = f(norm_ffn_post)[l]
    n_b = sum(1 for t in layer_types if t == "B")
    rp = f(rpb_b)
    tabs = []
    var_of = kts_of = None
    for jb in range(max(n_b, 1)):
        tb, var_of, kts_of = build_biasB(rp[jb], S)
        tabs.append(tb)
    biasB = np.ascontiguousarray(np.concatenate(tabs, axis=0))
    shared = dict(
        w_in_a=np.ascontiguousarray(f(w_in_a)[:, :, QPERM]), w_in_b=f(w_in_b), w_mem_kv=f(w_mem_kv), w_out=f(w_out), w_gu=f(w_gu), w_down=f(w_down),
        gcols=gc, gpost=gp, sink=f(sink_a), biasA=build_biasA(f(t5_table)), biasB=biasB,
        ident=np.eye(128, dtype=np.float32).astype(ml_dtypes.bfloat16),
    )
    return shared, biasB.shape[0] // max(n_b, 1), var_of, kts_of


_CACHE = {}
_QH = [0, 3, 1, 4, 2, 5, 6, 9, 7, 10, 8, 11]
QPERM = np.concatenate([np.arange(h * 64, (h + 1) * 64) for h in _QH] + [np.arange(768, 1536)])


def run_module(S, layer_types, inputs, n_cores=N_CORES):
    n_a = sum(1 for t in layer_types if t == "A")
    n_b = sum(1 for t in layer_types if t == "B")
    shared, nvarB, var_of, kts_of = make_inputs(S, layer_types, **inputs)
    key = (S, tuple(layer_types))
    if key not in _CACHE:
        _CACHE[key] = build_program(S, layer_types, n_a, n_b, nvarB, var_of, kts_of)
    nc, stats, hi = _CACHE[key]
    if VERBOSE:
        print("build stats", stats, "arena bytes", hi, flush=True)
    x = np.asarray(inputs["x"], dtype=np.float32)
    mem = np.asarray(inputs["mem"], dtype=np.float32)
    in_maps = []
    for c in range(n_cores):
        m = dict(shared)
        m["x"] = np.ascontiguousarray(x[c])
        m["mem"] = np.ascontiguousarray(mem[c])
        in_maps.append(m)
    res = run_bass_kernel_spmd(nc, in_maps, core_ids=list(range(n_cores)))
    return np.stack([np.asarray(r["out"]) for r in res.results], axis=0)


def kernel(x, mem, w_in_a, sink_a, w_in_b, rpb_b, t5_table, w_mem_kv, w_out, w_gu, w_down,
           norm_mix_pre, norm_mix_post, norm_mem, norm_ffn_pre, norm_ffn_post):
    inputs = dict(x=x, mem=mem, w_in_a=w_in_a, sink_a=sink_a, w_in_b=w_in_b, rpb_b=rpb_b, t5_table=t5_table,
                  w_mem_kv=w_mem_kv, w_out=w_out, w_gu=w_gu, w_down=w_down, norm_mix_pre=norm_mix_pre,
                  norm_mix_post=norm_mix_post, norm_mem=norm_mem, norm_ffn_pre=norm_ffn_pre,
                  norm_ffn_post=norm_ffn_post)
    S = np.asarray(x).shape[1]
    out = run_module(S, ["A", "B", "A", "B"], inputs)
    return out.astype(np.float32)
```
